# Optimizing a Trainium2 kernel written in Bass

```python
import numpy as np
import jax, jax.numpy as jnp
from jax import lax

D_MODEL = 1024
BATCH = 8
SEQ = 2048
DEPTH = 1
DEC_BATCH = 128
DEC_SEQ = 4
PAST_LEN = 16384
PAGE_SIZE = 128

PLE_DIM = 256
RET_HEADS = 4
RET_DK = 256
RET_DV = 512
RET_CHUNK = 128
MLA_HEADS = 8
MLA_NOPE = 128
MLA_ROPE = 64
MLA_V = 128
Q_LORA = 384
KV_LORA = 256
Q_BLOCK = 128
ROPE_BASE = 10000.0
EPS = 1e-6

RET_QK = RET_HEADS * RET_DK
RET_VW = RET_HEADS * RET_DV
MLA_QK = MLA_NOPE + MLA_ROPE
MLA_VW = MLA_HEADS * MLA_V
SPLIT_SIZES = (RET_QK, RET_QK, RET_VW, RET_VW, Q_LORA, KV_LORA, MLA_ROPE, MLA_VW, D_MODEL, D_MODEL)
IN_WIDTH = 2 * RET_QK + 2 * RET_VW + Q_LORA + KV_LORA + MLA_ROPE + MLA_VW + 2 * D_MODEL

kernel_name = 'hybrid_retention_mla_gated_decode_step'


def rmsnorm(x, w):
    xf = x.astype(jnp.float32)
    xf = xf * lax.rsqrt(jnp.mean(xf * xf, axis=-1, keepdims=True) + EPS)
    return xf.astype(x.dtype) * w


def rope(x, pos):
    half = x.shape[-1] // 2
    inv_freq = ROPE_BASE ** (-jnp.arange(half, dtype=jnp.float32) / half)
    ang = (pos[:, None] * inv_freq[None, :]).reshape((pos.shape[0],) + (1,) * (x.ndim - 3) + (half,))
    cos, sin = jnp.cos(ang), jnp.sin(ang)
    xf = x.astype(jnp.float32)
    x1, x2 = xf[..., :half], xf[..., half:]
    return jnp.concatenate([x1 * cos - x2 * sin, x2 * cos + x1 * sin], axis=-1).astype(x.dtype)


def retention_log_decay():
    return jnp.log1p(-jnp.exp2(-5.0 - jnp.arange(RET_HEADS, dtype=jnp.float32)))


def retention_chunk(q, k, v, s_prev, log_g):
    c = q.shape[1]
    idx = jnp.arange(c, dtype=jnp.float32)
    diff = idx[:, None] - idx[None, :]
    causal = (diff >= 0)[None]
    dmask = jnp.where(causal, jnp.exp(jnp.where(causal, diff[None], 0.0) * log_g[:, None, None]), 0.0)
    scores = jnp.einsum('bihd,bjhd->bhij', q, k) * dmask[None]
    intra = jnp.einsum('bhij,bjhe->bihe', scores, v)
    q_decay = jnp.exp((idx[:, None] + 1.0) * log_g[None, :])
    cross = jnp.einsum('bihd,bhde->bihe', q, s_prev) * q_decay[None, :, :, None]
    k_decay = jnp.exp((c - 1.0 - idx)[:, None] * log_g[None, :])
    s_new = (jnp.exp(c * log_g)[None, :, None, None] * s_prev
             + jnp.einsum('bjhd,bjhe->bhde', k * k_decay[None, :, :, None], v))
    return intra + cross, s_new


def retention_prompt(q, k, v, log_g):
    b, s, h, _ = q.shape
    nc = s // RET_CHUNK

    def to_chunks(t):
        return jnp.moveaxis(t.reshape(b, nc, RET_CHUNK, h, t.shape[-1]), 1, 0)

    def step(state, qkv):
        o, state = retention_chunk(qkv[0], qkv[1], qkv[2], state, log_g)
        return state, o

    s0 = jnp.zeros((b, h, RET_DK, RET_DV), jnp.float32)
    s_fin, o = lax.scan(step, s0, (to_chunks(q), to_chunks(k), to_chunks(v)))
    return jnp.moveaxis(o, 0, 1).reshape(b, s, h, RET_DV), s_fin


def retention_out(o, rg, gn_w):
    b, s = o.shape[0], o.shape[1]
    mu = jnp.mean(o, axis=-1, keepdims=True)
    var = jnp.mean(jnp.square(o - mu), axis=-1, keepdims=True)
    on = ((o - mu) * lax.rsqrt(var + EPS)).reshape(b, s, RET_VW).astype(rg.dtype)
    return on * gn_w * jax.nn.silu(rg)


def mixer_inputs(x, pos, ln_w, w_in, q_norm_w, w_uq, kv_norm_w):
    b, s, _ = x.shape
    z = rmsnorm(x, ln_w) @ w_in
    rq, rk, rv, rg, cq, ckv, kr, mg, ga, gb = jnp.split(z, np.cumsum(SPLIT_SIZES)[:-1].tolist(), axis=-1)
    rq = rope(rq.reshape(b, s, RET_HEADS, RET_DK), pos)
    rk = rope(rk.reshape(b, s, RET_HEADS, RET_DK), pos) * (RET_DK ** -0.5)
    rv = rv.reshape(b, s, RET_HEADS, RET_DV)
    q = (rmsnorm(cq, q_norm_w) @ w_uq).reshape(b, s, MLA_HEADS, MLA_QK)
    q_nope = q[..., :MLA_NOPE]
    q_rope = rope(q[..., MLA_NOPE:], pos)
    ckv = rmsnorm(ckv, kv_norm_w)
    kr = rope(kr, pos)
    return rq, rk, rv, rg, q_nope, q_rope, ckv, kr, mg, ga, gb


def mla_prompt(q_nope, q_rope, ckv, kr, w_ukv):
    b, s = ckv.shape[0], ckv.shape[1]
    kv = (ckv @ w_ukv).reshape(b, s, MLA_HEADS, MLA_NOPE + MLA_V)
    k_nope, v = kv[..., :MLA_NOPE], kv[..., MLA_NOPE:]
    nb = s // Q_BLOCK
    kpos = jnp.arange(s)
    scale = MLA_QK ** -0.5

    def block(args):
        qn, qr, i = args
        sc = jnp.einsum('bqhd,bkhd->bhqk', qn, k_nope) + jnp.einsum('bqhr,bkr->bhqk', qr, kr)
        sc = sc.astype(jnp.float32) * scale
        qpos = i * Q_BLOCK + jnp.arange(Q_BLOCK)
        sc = jnp.where(kpos[None, :] <= qpos[:, None], sc, -jnp.inf)
        p = jax.nn.softmax(sc, axis=-1).astype(v.dtype)
        return jnp.einsum('bhqk,bkhe->bqhe', p, v)

    qn_b = jnp.moveaxis(q_nope.reshape(b, nb, Q_BLOCK, MLA_HEADS, MLA_NOPE), 1, 0)
    qr_b = jnp.moveaxis(q_rope.reshape(b, nb, Q_BLOCK, MLA_HEADS, MLA_ROPE), 1, 0)
    o = lax.map(block, (qn_b, qr_b, jnp.arange(nb)))
    return jnp.moveaxis(o, 0, 1).reshape(b, s, MLA_VW)


def mla_sample(q_nope, q_rope, ckv, kr, cache_ckv, cache_kr, page_table, w_ukv):
    b, t = ckv.shape[0], ckv.shape[1]
    w = w_ukv.reshape(KV_LORA, MLA_HEADS, MLA_NOPE + MLA_V)
    w_uk, w_uv = w[..., :MLA_NOPE], w[..., MLA_NOPE:]
    q_lat = jnp.einsum('bthn,chn->bthc', q_nope, w_uk)
    causal = jnp.arange(t)[None, :] <= jnp.arange(t)[:, None]
    scale = MLA_QK ** -0.5

    def one_seq(args):
        ql, qr, pt, c_new, r_new = args
        c_past = cache_ckv[pt].reshape(-1, KV_LORA)
        r_past = cache_kr[pt].reshape(-1, MLA_ROPE)
        s_past = jnp.einsum('thc,kc->htk', ql, c_past) + jnp.einsum('thr,kr->htk', qr, r_past)
        s_new = jnp.einsum('thc,kc->htk', ql, c_new) + jnp.einsum('thr,kr->htk', qr, r_new)
        s_new = jnp.where(causal[None], s_new.astype(jnp.float32) * scale, -jnp.inf)
        sc = jnp.concatenate([s_past.astype(jnp.float32) * scale, s_new], axis=-1)
        p = jax.nn.softmax(sc, axis=-1).astype(c_new.dtype)
        n_past = c_past.shape[0]
        return (jnp.einsum('htk,kc->thc', p[..., :n_past], c_past)
                + jnp.einsum('htk,kc->thc', p[..., n_past:], c_new))

    o_lat = lax.map(one_seq, (q_lat, q_rope, page_table, ckv, kr))
    return jnp.einsum('bthc,chv->bthv', o_lat, w_uv).reshape(b, t, MLA_VW)


def finish_layer(x, ret_o, rg, mla_o, mg, ga, gb, ple, ret_gn_w, w_down_ret, w_down_mla, w_out, w_ple_gate, w_ple_proj):
    h_ret = retention_out(ret_o, rg, ret_gn_w) @ w_down_ret
    h_mla = (mla_o * jax.nn.silu(mg)) @ w_down_mla
    merged = jax.nn.sigmoid(ga) * h_ret + jax.nn.sigmoid(gb) * h_mla
    x = x + merged @ w_out
    return x + jax.nn.sigmoid(x @ w_ple_gate) * (ple @ w_ple_proj)


def setup_inputs(seed: int = 0) -> dict:
    key = jax.random.key(seed)
    ks = jax.random.split(key, 24)
    n_pages = PAST_LEN // PAGE_SIZE
    n_used = DEC_BATCH * n_pages
    n_pool = n_used + n_used // 4

    def w(k, shape, fan_in):
        return jax.random.normal(k, shape, jnp.float32) * (fan_in ** -0.5)

    def gain(k, shape):
        return 1.0 + 0.01 * jax.random.normal(k, shape, jnp.float32)

    page_table = jax.random.permutation(ks[0], n_pool)[:n_used].reshape(DEC_BATCH, n_pages).astype(jnp.int32)
    return {
        'x_prompt': jax.random.normal(ks[1], (BATCH, SEQ, D_MODEL), jnp.float32),
        'x_sample': jax.random.normal(ks[2], (DEC_BATCH, DEC_SEQ, D_MODEL), jnp.float32),
        'cache_ckv': jax.random.normal(ks[3], (DEPTH, n_pool, PAGE_SIZE, KV_LORA), jnp.float32),
        'cache_krope': jax.random.normal(ks[4], (DEPTH, n_pool, PAGE_SIZE, MLA_ROPE), jnp.float32),
        'state_ret': 0.5 * jax.random.normal(ks[5], (DEPTH, DEC_BATCH, RET_HEADS, RET_DK, RET_DV), jnp.float32),
        'page_table': page_table,
        'p_prompt': jax.random.normal(ks[6], (DEPTH, BATCH, SEQ, PLE_DIM), jnp.float32),
        'p_sample': jax.random.normal(ks[7], (DEPTH, DEC_BATCH, DEC_SEQ, PLE_DIM), jnp.float32),
        'ln_w': gain(ks[8], (DEPTH, D_MODEL)),
        'w_in': w(ks[9], (DEPTH, D_MODEL, IN_WIDTH), D_MODEL),
        'q_norm_w': gain(ks[10], (DEPTH, Q_LORA)),
        'w_uq': w(ks[11], (DEPTH, Q_LORA, MLA_HEADS * MLA_QK), Q_LORA),
        'kv_norm_w': gain(ks[12], (DEPTH, KV_LORA)),
        'w_ukv': w(ks[13], (DEPTH, KV_LORA, MLA_HEADS * (MLA_NOPE + MLA_V)), KV_LORA),
        'ret_gn_w': gain(ks[14], (DEPTH, RET_VW)),
        'w_down_ret': w(ks[15], (DEPTH, RET_VW, D_MODEL), RET_VW),
        'w_down_mla': w(ks[16], (DEPTH, MLA_VW, D_MODEL), MLA_VW),
        'w_out': w(ks[17], (DEPTH, D_MODEL, D_MODEL), D_MODEL),
        'w_ple_gate': w(ks[18], (DEPTH, D_MODEL, D_MODEL), D_MODEL),
        'w_ple_proj': w(ks[19], (DEPTH, PLE_DIM, D_MODEL), PLE_DIM),
        'final_norm_w': gain(ks[20], (D_MODEL,)),
    }


def reference(x_prompt, x_sample, cache_ckv, cache_krope, state_ret, page_table, p_prompt, p_sample,
              ln_w, w_in, q_norm_w, w_uq, kv_norm_w, w_ukv, ret_gn_w, w_down_ret, w_down_mla, w_out,
              w_ple_gate, w_ple_proj, final_norm_w):
    log_g = retention_log_decay()
    past_len = page_table.shape[1] * cache_ckv.shape[2]
    pos_p = jnp.arange(x_prompt.shape[1], dtype=jnp.float32)
    pos_s = past_len + jnp.arange(x_sample.shape[1], dtype=jnp.float32)
    y_p, y_s = x_prompt, x_sample
    ckv_p_l, kr_p_l, ret_p_l, ckv_s_l, kr_s_l, ret_s_l = [], [], [], [], [], []
    for i in range(DEPTH):
        rq, rk, rv, rg, qn, qr, ckv, kr, mg, ga, gb = mixer_inputs(y_p, pos_p, ln_w[i], w_in[i], q_norm_w[i], w_uq[i], kv_norm_w[i])
        ret_o, s_fin = retention_prompt(rq.astype(jnp.float32), rk.astype(jnp.float32), rv.astype(jnp.float32), log_g)
        mla_o = mla_prompt(qn, qr, ckv, kr, w_ukv[i])
        y_p = finish_layer(y_p, ret_o, rg, mla_o, mg, ga, gb, p_prompt[i], ret_gn_w[i], w_down_ret[i], w_down_mla[i],
                           w_out[i], w_ple_gate[i], w_ple_proj[i])
        ckv_p_l.append(ckv)
        kr_p_l.append(kr)
        ret_p_l.append(s_fin.astype(x_prompt.dtype))
        rq, rk, rv, rg, qn, qr, ckv, kr, mg, ga, gb = mixer_inputs(y_s, pos_s, ln_w[i], w_in[i], q_norm_w[i], w_uq[i], kv_norm_w[i])
        ret_o, s_new = retention_chunk(rq.astype(jnp.float32), rk.astype(jnp.float32), rv.astype(jnp.float32),
                                       state_ret[i].astype(jnp.float32), log_g)
        mla_o = mla_sample(qn, qr, ckv, kr, cache_ckv[i], cache_krope[i], page_table, w_ukv[i])
        y_s = finish_layer(y_s, ret_o, rg, mla_o, mg, ga, gb, p_sample[i], ret_gn_w[i], w_down_ret[i], w_down_mla[i],
                           w_out[i], w_ple_gate[i], w_ple_proj[i])
        ckv_s_l.append(ckv)
        kr_s_l.append(kr)
        ret_s_l.append(s_new.astype(state_ret.dtype))
    y_prompt = rmsnorm(y_p, final_norm_w)
    y_sample = rmsnorm(y_s, final_norm_w)
    ckv_prompt = jnp.stack(ckv_p_l)
    krope_prompt = jnp.stack(kr_p_l)
    ret_prompt = jnp.stack(ret_p_l)
    ckv_sample = jnp.stack(ckv_s_l)
    krope_sample = jnp.stack(kr_s_l)
    ret_sample = jnp.stack(ret_s_l)
    return (y_prompt, y_sample, ckv_prompt, krope_prompt, ret_prompt, ckv_sample, krope_sample, ret_sample)
```

```python
import contextlib
import os
import numpy as np
import concourse.bass as bass
import concourse.mybir as mybir
from concourse.bass_utils import run_bass_kernel_spmd

F32 = mybir.dt.float32
BF16 = mybir.dt.bfloat16
I32 = mybir.dt.int32
AF = mybir.ActivationFunctionType
ALU = mybir.AluOpType
AX = mybir.AxisListType

D = 1024
EPS = 1e-6
SCALE = 192 ** -0.5
NEG = -30000.0
C_RQ, C_RK, C_RV, C_RG, C_CQ, C_CKV, C_KR, C_MG, C_GA, C_GB = 0, 1024, 2048, 4096, 6144, 6528, 6784, 6848, 7872, 8896
INW = 9920


class Buf:
    __slots__ = ("name", "lw", "rd")

    def __init__(self, name=""):
        self.name = name
        self.lw = None
        self.rd = []


class Op:
    __slots__ = ("eng", "fn", "kind", "eidx", "waits", "signal", "sigval", "sem", "semval", "prewait", "waitall")


ENGS = ("pe", "act", "dve", "pool", "sp")
NQ = 8


class Sched:
    def __init__(self, nc):
        self.nc = nc
        self.eops = {e: [] for e in ENGS}
        self.ndma = {e: 0 for e in ENGS}
        self.seen = {e: {p: -1 for p in ENGS} for e in ENGS}
        self.seen_dma = {e: set() for e in ENGS}

    def op(self, eng, fn, reads=(), writes=(), kind="c"):
        o = Op()
        o.eng, o.fn, o.kind = eng, fn, kind
        o.waits, o.signal, o.sigval, o.sem, o.semval, o.prewait = [], False, 0, None, 0, None
        o.waitall = None
        o.eidx = len(self.eops[eng])
        self.eops[eng].append(o)
        deps = []
        for b in reads:
            if b.lw is not None:
                deps.append(b.lw)
        for b in writes:
            if b.lw is not None:
                deps.append(b.lw)
            deps.extend(b.rd)
        for b in writes:
            b.lw = o
            b.rd = []
        for b in reads:
            if b.lw is not o:
                b.rd.append(o)
        seen = self.seen[eng]
        for d in deps:
            if d is o:
                continue
            if d.kind == "d":
                if id(d) in self.seen_dma[eng]:
                    continue
                self.seen_dma[eng].add(id(d))
                o.waits.append(d)
            else:
                if d.eng == eng and eng == "pe":
                    continue
                if d.eidx <= seen[d.eng]:
                    continue
                seen[d.eng] = d.eidx
                d.signal = True
                o.waits.append(d)
        if kind == "d":
            j = self.ndma[eng]
            self.ndma[eng] = j + 1
            o.sem = j % NQ
            o.semval = 16 * (j // NQ + 1)
            if j >= NQ:
                o.prewait = (j % NQ, 16 * (j // NQ))
        return o

    def dma(self, eng, fn, reads=(), writes=()):
        return self.op(eng, fn, reads, writes, kind="d")

    def barrier(self, tiny1, tiny2):
        if not hasattr(self, "bar"):
            self.bar = {e: Buf("bar" + e) for e in ENGS}
            self.bar2 = {e: Buf("bar2" + e) for e in ENGS}
        for e in ENGS:
            nd = self.ndma[e]
            o = self.op(e, tiny1[e], reads=(), writes=(self.bar[e],), kind=("d" if e == "sp" else "c"))
            o.waitall = nd
        for e in ENGS:
            self.op(e, tiny2[e], reads=[self.bar[x] for x in ENGS], writes=(self.bar2[e],), kind=("d" if e == "sp" else "c"))

    def emit(self):
        nc = self.nc
        with contextlib.ExitStack() as st:
            esem = {e: st.enter_context(nc.semaphore(f"s_{e}")) for e in ENGS}
            dsem = {e: [st.enter_context(nc.semaphore(f"d_{e}{i}")) for i in range(NQ)]
                    for e in ENGS if self.ndma[e] > 0}
            for e in ENGS:
                c = 0
                for o in self.eops[e]:
                    if o.kind == "c" and o.signal:
                        c += 1
                        o.sigval = c
            block = st.enter_context(nc.Block())

            def run(e, eng):
                for o in self.eops[e]:
                    if o.waitall is not None:
                        for j in range(max(0, o.waitall - NQ), o.waitall):
                            eng.wait_ge(dsem[e][j % NQ], 16 * (j // NQ + 1))
                    if o.prewait is not None:
                        eng.wait_ge(dsem[e][o.prewait[0]], o.prewait[1])
                    for d in o.waits:
                        if d.kind == "d":
                            eng.wait_ge(dsem[d.eng][d.sem], d.semval)
                        else:
                            eng.wait_ge(esem[d.eng], d.sigval)
                    inst = o.fn(eng)
                    if o.kind == "d":
                        inst.then_inc(dsem[e][o.sem], 16)
                    elif o.signal:
                        inst.then_inc(esem[e], 1)
                n = self.ndma[e]
                for j in range(max(0, n - NQ), n):
                    eng.wait_ge(dsem[e][j % NQ], 16 * (j // NQ + 1))

            @block.tensor
            def _(eng):
                run("pe", eng)

            @block.scalar
            def _(eng):
                run("act", eng)

            @block.vector
            def _(eng):
                run("dve", eng)

            @block.gpsimd
            def _(eng):
                run("pool", eng)

            @block.sync
            def _(eng):
                run("sp", eng)


def splits(n, step):
    return [(o, min(step, n - o)) for o in range(0, n, step)]


def build(SEQ, NSQ, NPG, NPOOL, TB):
    NS = NSQ * 4
    TT = SEQ + NS
    CH = min(8, NPG)
    NCH = NPG // CH
    assert NPG % CH == 0 and SEQ % TB == 0 and TB % 128 == 0
    TBM = max(TB, NS)
    NTB = (TBM + 127) // 128
    nc = bass.Bass("TRN2", target_bir_lowering=False)

    def din(name, shape, dt=F32):
        return nc.dram_tensor(name, shape, dt, kind="ExternalInput").ap()

    def dout(name, shape):
        return nc.dram_tensor(name, shape, F32, kind="ExternalOutput").ap()

    xp = din("xp", [SEQ, D]); xs = din("xs", [NS, D])
    cck = din("cck", [NPOOL * 16, 8 * 256]); ckr = din("ckr", [NPOOL * 16, 8 * 64])
    stt = din("stt", [NSQ * 4 * 256, 512])
    ptab = din("ptab", [1, NSQ * NPG], I32)
    ppl = din("ppl", [SEQ, 256]); psl = din("psl", [NS, 256])
    ln_w = din("ln_w", [D]); w_in = din("w_in", [D, INW]); q_norm_w = din("q_norm_w", [384])
    w_uq = din("w_uq", [384, 1536]); kv_norm_w = din("kv_norm_w", [1, 256]); w_ukv = din("w_ukv", [256, 2048])
    ret_gn_w = din("ret_gn_w", [2048]); w_down_ret = din("w_down_ret", [2048, D])
    w_down_mla = din("w_down_mla", [D, D]); w_out = din("w_out", [D, D])
    w_ple_gate = din("w_ple_gate", [D, D]); w_ple_proj = din("w_ple_proj", [256, D])
    fin_w = din("fin_w", [1, D])
    k_cR = din("k_cR", [128, TT]); k_sR = din("k_sR", [128, TT])
    k_c2 = din("k_c2", [64, TT]); k_s2 = din("k_s2", [64, TT])
    k_cM = din("k_cM", [TT, 32]); k_sM = din("k_sM", [TT, 32])
    k_id = din("k_id", [128, 128])
    k_dm = din("k_dm", [128, 4, 128]); k_dms = din("k_dms", [NS, 4, NS])
    k_qd = din("k_qd", [128, 4, 128]); k_qds = din("k_qds", [128, 4, NS])
    k_kd = din("k_kd", [128, 4]); k_kds = din("k_kds", [NS, 4])
    k_bm = din("k_bm", [128, NSQ, NS]); k_rm = din("k_rm", [NS, NSQ])
    k_cm = din("k_cm", [128, 128]); k_sm = din("k_sm", [32, NSQ, NS])
    k_io = din("k_io", [128, 1], I32)

    yp = dout("yp", [SEQ, D]); ys = dout("ys", [NS, D])
    ockp = dout("ockp", [SEQ, 256]); okrp = dout("okrp", [SEQ, 64])
    oretp = dout("oretp", [4 * 256, 512])
    ocks = dout("ocks", [NS, 256]); okrs = dout("okrs", [NS, 64])
    orets = dout("orets", [NSQ * 4 * 256, 512])

    S = Sched(nc)
    LG = [float(np.log1p(-np.exp2(-5.0 - h))) for h in range(4)]
    WSRC = {"w_in": w_in, "w_down_ret": w_down_ret, "w_down_mla": w_down_mla, "w_out": w_out,
            "w_ple_gate": w_ple_gate, "w_ple_proj": w_ple_proj}
    WSCR = {k: nc.dram_tensor(k + "_b", list(v.shape), BF16, kind="Internal").ap() for k, v in WSRC.items()}
    conv_done = {}
    conv_pending = {}

    def conv_register(name, col0, ncols):
        conv_pending[(name, col0, ncols)] = None

    def conv_emit(key):
        if key in conv_done:
            return conv_done[key]
        conv_pending.pop(key, None)
        name, col0, ncols = key
        B = Buf(str(key))
        S.dma("pool", lambda e: e.dma_start(out=WSCR[name][:, col0:col0 + ncols], in_=WSRC[name][:, col0:col0 + ncols]), writes=[B])
        conv_done[key] = B
        return B

    def conv_emit_some(n):
        for _ in range(n):
            if conv_pending:
                conv_emit(next(iter(conv_pending)))

    conv_register("w_in", C_CQ, 384)
    conv_register("w_in", C_CKV, 320)
    for h in range(8):
        conv_register("w_in", C_MG + h * 128, 128)
    for h in range(4):
        conv_register("w_in", C_RQ + h * 256, 256)
        conv_register("w_in", C_RK + h * 256, 256)
        conv_register("w_in", C_RV + h * 512, 512)
        conv_register("w_in", C_RG + h * 512, 512)
    for cb in range(8):
        conv_register("w_down_ret", cb * 128, 128)
        conv_register("w_down_mla", cb * 128, 128)
        conv_register("w_in", C_GA + cb * 128, 128)
        conv_register("w_in", C_GB + cb * 128, 128)
    conv_register("w_out", 0, D)
    conv_register("w_ple_gate", 0, D)
    conv_register("w_ple_proj", 0, D)

    def wdma(dst, name, col0, ncols, reads=(), writes=()):
        B = conv_emit((name, col0, ncols))
        S.dma("sp", lambda e: e.dma_start(out=dst, in_=WSCR[name][:, col0:col0 + ncols].rearrange("(c p) n -> p c n", p=128)),
              reads=[B] + list(reads), writes=list(writes))

    with contextlib.ExitStack() as st:
        cnt = [0]
        stk = [st]

        def sb(shape, dt, name=None):
            cnt[0] += 1
            return stk[-1].enter_context(nc.sbuf_tensor(f"{name or 't'}_{cnt[0]}", shape, dt))

        class RP:
            def __init__(self, shape, dt, n, name):
                self.t = [sb(shape, dt, f"{name}{i}") for i in range(n)]
                self.b = [Buf(f"{name}{i}") for i in range(n)]
                self.i = 0

            def get(self):
                k = self.i % len(self.t)
                self.i += 1
                return self.t[k], self.b[k]

        pbank = [st.enter_context(nc.psum_tensor(f"pb{i}", [128, 512], F32)) for i in range(8)]
        pbuf = [Buf(f"pb{i}") for i in range(8)]
        pidx = [0]

        ps_avoid = []

        def PS():
            k = pidx[0] % 8
            pidx[0] += 1
            while pbuf[k] in ps_avoid:
                k = pidx[0] % 8
                pidx[0] += 1
            return pbank[k], pbuf[k]

        def bfv(p):
            return p[:].bitcast(BF16)

        def MM(out, pairs):
            def f(e):
                n = len(pairs)
                r = None
                for i, (l, rh) in enumerate(pairs):
                    r = e.matmul(out, lhsT=l, rhs=rh, start=(i == 0), stop=(i == n - 1))
                return r
            return f

        def MMS(groups):
            def f(e):
                r = None
                for out, pairs in groups:
                    n = len(pairs)
                    for i, (l, rh) in enumerate(pairs):
                        r = e.matmul(out, lhsT=l, rhs=rh, start=(i == 0), stop=(i == n - 1))
                return r
            return f

        def TR(items, idn):
            def f(e):
                r = None
                for o, i in items:
                    k = i.shape[0]
                    r = e.transpose(o, i, idn[:k, :k])
                return r
            return f

        def ACT(out, in_, func=AF.Copy, **kw):
            return lambda e: e.activation(out=out, in_=in_, func=func, **kw)

        def TT_(out, a, b, op):
            return lambda e: e.tensor_tensor(out=out, in0=a, in1=b, op=op)

        def TS(out, a, s1, s2, op0, op1=None):
            if op1 is None:
                return lambda e: e.tensor_scalar(out=out, in0=a, scalar1=s1, scalar2=None, op0=op0)
            return lambda e: e.tensor_scalar(out=out, in0=a, scalar1=s1, scalar2=s2, op0=op0, op1=op1)

        def STT(out, a, s, b, op0, op1):
            return lambda e: e.scalar_tensor_tensor(out=out, in0=a, scalar=s, in1=b, op0=op0, op1=op1)

        def DMA(out, in_, **kw):
            return lambda e: e.dma_start(out=out, in_=in_, **kw)

        def CP(out, in_):
            return lambda e: e.tensor_copy(out=out, in_=in_)

        pending_st = []

        def store(out, in_, reads):
            pending_st.append((DMA(out, in_), list(reads)))

        def flush_st():
            for fn, rd in pending_st:
                S.dma("sp", fn, reads=rd)
            del pending_st[:]

        def RECIP(out, in_):
            return lambda e: e.reciprocal(out=out, in_=in_)

        def RED(out, in_, op):
            return lambda e: e.tensor_reduce(out=out, in_=in_, axis=AX.X, op=op)

        def MSET(out, v):
            return lambda e: e.memset(out, v)

        Bc = Buf("consts")

        def cload(shape, dt, src, eng="sp", **kw):
            t = sb(shape, dt)
            S.dma(eng, DMA(t[:], src, **kw), writes=[Bc])
            return t

        KCUT = int(os.environ.get("KCUT", "99"))
        idb = cload([128, 128], BF16, k_id, "pool")
        dm = cload([128, 4, 128], F32, k_dm)
        qd = cload([128, 4, 128], F32, k_qd)
        kd = cload([128, 4], F32, k_kd)
        cm = cload([128, 128], F32, k_cm)
        lnwT = cload([128, 8], F32, ln_w.rearrange("(c p) -> p c", p=128), allow_slow_non_contiguous=True)
        qnwT = cload([128, 3], F32, q_norm_w.rearrange("(c p) -> p c", p=128), allow_slow_non_contiguous=True)
        gnwT = cload([128, 16], F32, ret_gn_w.rearrange("(c p) -> p c", p=128), allow_slow_non_contiguous=True)
        kvwB = cload([128, 256], F32, kv_norm_w.partition_broadcast(128))
        finB = cload([128, D], F32, fin_w.partition_broadcast(128))
        ones = sb([128, 128], BF16)
        S.op("pool", MSET(ones[:], 1.0), writes=[Bc])
        scr = sb([128, 16], F32); Bscr = Buf("scr")
        S.op("pool", MSET(scr[:], 0.0), writes=[Bscr])
        wuq = sb([128, 3, 1536], BF16)
        S.dma("pool", DMA(wuq[:], w_uq.rearrange("(c p) n -> p c n", p=128)), writes=[Bc])
        wuqs = sb([128, 3, 8, 64], BF16)
        uq4 = w_uq.rearrange("(c p) (h f) -> p c h f", p=128, f=192)
        for c in range(3):
            S.dma("pool", DMA(wuqs[:, c, :, 0:32], uq4[:, c, :, 160:192]), writes=[Bc])
            S.dma("pool", DMA(wuqs[:, c, :, 32:64], uq4[:, c, :, 128:160]), writes=[Bc])
        wukv = sb([128, 2, 2048], BF16)
        S.dma("pool", DMA(wukv[:], w_ukv.rearrange("(c p) n -> p c n", p=128)), writes=[Bc])

        xnT = sb([128, 8, TBM], BF16); BxnT = Buf("xnT")
        moT = sb([128, 8, TBM], BF16); BmoT = [Buf(f"moT{h}") for h in range(8)]
        gaT = sb([128, 16, TBM], BF16); BgaT = [Buf(f"gaT{h}") for h in range(4)]
        mrT = sb([128, 8, TBM], BF16); BmrT = Buf("mrT")
        cqn = sb([128, 3, TBM], BF16); Bcqn = Buf("cqn")
        tabR = sb([128, 2, TBM], BF16); tab2 = sb([64, 2, TBM], BF16); Btab = Buf("tab")
        SM1 = RP([128, 8], F32, 12, "sm1")
        F512 = RP([128, 2, 512], F32, 3, "f512")
        JK = RP([128, D], BF16, 1, "jk")
        XT = RP([128, D], F32, 2, "xt")
        XB = RP([128, D], BF16, 2, "xb")

        def do_barrier():
            if os.environ.get("KNOBAR"):
                return
            p, Bp = PS()
            t1 = {"pe": MM(p[:1, 0:1], [(ones[:1, :1], ones[:1, 0:1])]), "act": ACT(scr[:, 0:1], scr[:, 8:9]),
                  "dve": CP(scr[:, 1:2], scr[:, 9:10]), "pool": MSET(scr[:, 2:3], 0.0), "sp": DMA(scr[0:1, 3:4], scr[0:1, 10:11])}
            t2 = {"pe": MM(p[:1, 1:2], [(ones[:1, :1], ones[:1, 0:1])]), "act": ACT(scr[:, 4:5], scr[:, 11:12]),
                  "dve": CP(scr[:, 5:6], scr[:, 12:13]), "pool": MSET(scr[:, 6:7], 0.0), "sp": DMA(scr[0:1, 7:8], scr[0:1, 13:14])}
            S.barrier(t1, t2)

        @contextlib.contextmanager
        def phase():
            ph = contextlib.ExitStack()
            stk.append(ph)
            try:
                yield
                flush_st()
                do_barrier()
            finally:
                stk.pop()
                ph.close()

        KSTOP = os.environ.get("KSTOP", "")

        def block(kind, tok0, ntok, X):
            smp = kind == "s"
            stop = KSTOP.split(":")[1] if KSTOP.startswith(kind + ":") else "Z"
            if KSTOP and not KSTOP.startswith(kind + ":"):
                if not (KSTOP.startswith("p:") and smp and len(KSTOP.split(":")) > 2):
                    return
            tc0 = SEQ if smp else tok0
            xsrc = xs if smp else xp
            tiles = splits(ntok, 128)
            groups = splits(ntok, 512)

            def wload(col0, ncols, WB):
                w, Bw = WB.get()
                wdma(w[:, :, :ncols], "w_in", col0, ncols, writes=[Bw])
                return w, Bw

            if stop == "0":
                return
            S.dma("pool", DMA(tabR[:, 0, :ntok], k_cR[:, tc0:tc0 + ntok]), writes=[Btab])
            S.dma("pool", DMA(tabR[:, 1, :ntok], k_sR[:, tc0:tc0 + ntok]), writes=[Btab])
            S.dma("pool", DMA(tab2[:, 0, :ntok], k_c2[:, tc0:tc0 + ntok]), writes=[Btab])
            S.dma("pool", DMA(tab2[:, 1, :ntok], k_s2[:, tc0:tc0 + ntok]), writes=[Btab])

            for (o, n) in tiles:
                xt, Bxt = XT.get()
                S.dma("sp", DMA(xt[:n], xsrc[tok0 + o:tok0 + o + n, :]), writes=[Bxt])
                jk, Bjk = JK.get()
                s1, Bs1 = SM1.get()
                S.op("act", ACT(jk[:n], xt[:n], AF.Square, accum_out=s1[:n, 0:1]), reads=[Bxt], writes=[Bjk, Bs1])
                S.op("act", ACT(s1[:n, 1:2], s1[:n, 0:1], AF.Sqrt, scale=1.0 / D, bias=EPS), reads=[Bs1], writes=[Bs1])
                S.op("dve", RECIP(s1[:n, 2:3], s1[:n, 1:2]), reads=[Bs1], writes=[Bs1])
                xb, Bxb = XB.get()
                S.op("dve", TS(xb[:n], xt[:n], s1[:n, 2:3], None, ALU.mult), reads=[Bxt, Bs1], writes=[Bxb])
                p, Bp = PS()
                pv = bfv(p)
                S.op("pe", TR([(pv[:, c * 128:c * 128 + n], xb[:n, c * 128:(c + 1) * 128]) for c in range(8)], idb),
                     reads=[Bxb, Bc], writes=[Bp])
                S.op("dve", TT_(xnT[:, :, o:o + n], pv.rearrange("p (c t) -> p c t", c=8)[:, :, :n],
                                lnwT[:].unsqueeze(2).to_broadcast([128, 8, n]), ALU.mult),
                     reads=[Bp, Bc], writes=[BxnT])

            if stop == "A":
                return
            with phase():
                cqT = sb([128, 3, TBM], F32); BcqT = Buf("cqT")
                sqT = sb([128, 3, TBM], BF16); BsqT = Buf("sqT")
                CKO = RP([128, 320], F32, 2, "cko")
                CKB = RP([128, 320], BF16, 2, "ckb")
                CSP = RP([128, 64], F32, 2, "csp")
                WB = RP([128, 8, 512], BF16, 2, "wb")
                w, Bw = wload(C_CQ, 384, WB)
                for j in range(3):
                    for (o, n) in groups:
                        p, Bp = PS()
                        S.op("pe", MM(p[:, :n], [(w[:, k, j * 128:(j + 1) * 128], xnT[:, k, o:o + n]) for k in range(8)]),
                             reads=[Bw, BxnT], writes=[Bp])
                        S.op("act", ACT(cqT[:, j, o:o + n], p[:, :n]), reads=[Bp], writes=[BcqT])
                        S.op("act", ACT(sqT[:, j, o:o + n], p[:, :n], AF.Square), reads=[Bp], writes=[BsqT])
                for (o, n) in groups:
                    p, Bp = PS()
                    S.op("pe", MM(p[:, :n], [(ones[:], sqT[:, j, o:o + n]) for j in range(3)]), reads=[BsqT, Bc], writes=[Bp])
                    f5, Bf5 = F512.get()
                    S.op("act", ACT(f5[:, 0, :n], p[:, :n], AF.Sqrt, scale=1.0 / 384, bias=EPS), reads=[Bp], writes=[Bf5])
                    S.op("dve", RECIP(f5[:, 1, :n], f5[:, 0, :n]), reads=[Bf5], writes=[Bf5])
                    for j in range(3):
                        S.op("dve", STT(cqn[:, j, o:o + n], cqT[:, j, o:o + n], qnwT[:, j:j + 1], f5[:, 1, :n], ALU.mult, ALU.mult),
                             reads=[BcqT, Bf5, Bc], writes=[Bcqn])
                w, Bw = wload(C_CKV, 320, WB)
                for ti, (o, n) in enumerate(tiles):
                    p, Bp = PS()
                    S.op("pe", MM(p[:n, :320], [(xnT[:, k, o:o + n], w[:, k, :320]) for k in range(8)]),
                         reads=[Bw, BxnT], writes=[Bp])
                    ot, Bot = CKO.get()
                    jk, Bjk = JK.get()
                    s1, Bs1 = SM1.get()
                    S.op("act", ACT(jk[:n, :256], p[:n, :256], AF.Square, accum_out=s1[:n, 0:1]), reads=[Bp], writes=[Bjk, Bs1])
                    S.op("act", ACT(s1[:n, 1:2], s1[:n, 0:1], AF.Sqrt, scale=1.0 / 256, bias=EPS), reads=[Bs1], writes=[Bs1])
                    S.op("dve", RECIP(s1[:n, 2:3], s1[:n, 1:2]), reads=[Bs1], writes=[Bs1])
                    S.op("dve", STT(ot[:n, 0:256], p[:n, 0:256], s1[:n, 2:3], kvwB[:n, :], ALU.mult, ALU.mult),
                         reads=[Bp, Bs1, Bc], writes=[Bot])
                    cst, Bcst = CSP.get()
                    S.dma("sp", DMA(cst[:n, 0:32], k_cM[tc0 + o:tc0 + o + n, :]), writes=[Bcst])
                    S.dma("sp", DMA(cst[:n, 32:64], k_sM[tc0 + o:tc0 + o + n, :]), reads=[Bcst], writes=[Bcst])
                    flush_st()
                    cs_ = cst[:n, 0:32]
                    sn_ = cst[:n, 32:64]
                    f5, Bf5 = F512.get()
                    S.op("dve", TT_(f5[:n, 0, 0:32], p[:n, 256:288], cs_, ALU.mult), reads=[Bp, Bcst], writes=[Bf5])
                    S.op("dve", TT_(f5[:n, 0, 32:64], p[:n, 288:320], sn_, ALU.mult), reads=[Bp, Bcst, Bf5], writes=[Bf5])
                    S.op("dve", TT_(f5[:n, 0, 64:96], p[:n, 288:320], cs_, ALU.mult), reads=[Bp, Bcst, Bf5], writes=[Bf5])
                    S.op("dve", TT_(f5[:n, 0, 96:128], p[:n, 256:288], sn_, ALU.mult), reads=[Bp, Bcst, Bf5], writes=[Bf5])
                    S.op("dve", TT_(ot[:n, 256:288], f5[:n, 0, 0:32], f5[:n, 0, 32:64], ALU.subtract), reads=[Bf5, Bot], writes=[Bot])
                    S.op("dve", TT_(ot[:n, 288:320], f5[:n, 0, 64:96], f5[:n, 0, 96:128], ALU.add), reads=[Bf5, Bot], writes=[Bot])
                    if smp:
                        store(ocks[o:o + n, :], ot[:n, 0:256], [Bot])
                        store(okrs[o:o + n, :], ot[:n, 256:320], [Bot])
                    else:
                        store(ockp[tok0 + o:tok0 + o + n, :], ot[:n, 0:256], [Bot])
                        store(okrp[tok0 + o:tok0 + o + n, :], ot[:n, 256:320], [Bot])
                    ob, Bob = CKB.get()
                    S.op("act", ACT(ob[:n, :], ot[:n, :]), reads=[Bot], writes=[Bob])
                    if smp:
                        S.op("dve", CP(X["ckS"][:n, :], ob[:n, 0:256]), reads=[Bob], writes=[X["Bcks"]])
                    p2, Bp2 = PS()
                    pv2 = bfv(p2)
                    S.op("pe", TR([(pv2[:, 0:n], ob[:n, 0:128]), (pv2[:, 128:128 + n], ob[:n, 128:256]),
                                   (pv2[:64, 256:256 + n], ob[:n, 256:320])], idb), reads=[Bob, Bc], writes=[Bp2])
                    if smp:
                        S.op("act", ACT(X["ckTs"][:, :, :n], pv2[:, 0:256].rearrange("p (c t) -> p c t", c=2)[:, :, :n]), reads=[Bp2], writes=[X["Bcks"]])
                        S.op("act", ACT(X["krTs"][:, :n], pv2[:64, 256:256 + n]), reads=[Bp2, X["Bcks"]], writes=[X["Bcks"]])
                    else:
                        gt = (tok0 + o) // 128
                        S.op("act", ACT(X["ckT"][:, :, tok0 + o:tok0 + o + n], pv2[:, 0:256].rearrange("p (c t) -> p c t", c=2)[:, :, :n]),
                             reads=[Bp2], writes=[X["BckT"][gt]])
                        S.op("act", ACT(X["krT"][:, tok0 + o:tok0 + o + n], pv2[:64, 256:256 + n]), reads=[Bp2], writes=[X["BkrT"][gt]])

            if stop == "B":
                return
            with phase():
                QN = RP([128, TBM], BF16, 2, "qn")
                QR = RP([64, TBM], BF16, 2, "qr")
                WM = RP([128, 8, 128], BF16, 2, "wm")
                MG = RP([128, TBM], BF16, 2, "mg")
                ON = RP([128, 128], BF16, 2, "on")
                if smp:
                    sample_attn = make_sample_attn(X)
                else:
                    KN = RP([128, SEQ], BF16, 1, "kn")
                    VH = RP([128, SEQ // 128, 128], BF16, 1, "vh")
                    PM = RP([128, SEQ], BF16, 1, "pm")
                    PTT = RP([128, SEQ // 128, 128], BF16, 1, "ptt")
                mgs = []
                for h in range(8):
                    wm, Bwm = WM.get()
                    wdma(wm[:], "w_in", C_MG + h * 128, 128, writes=[Bwm])
                    mg, Bmg = MG.get()
                    for (o, n) in groups:
                        p, Bp = PS()
                        S.op("pe", MM(p[:, :n], [(wm[:, k, :], xnT[:, k, o:o + n]) for k in range(8)]), reads=[Bwm, BxnT], writes=[Bp])
                        S.op("act", ACT(mg[:, o:o + n], p[:, :n], AF.Silu), reads=[Bp], writes=[Bmg])
                    qn, Bqn = QN.get()
                    qr, Bqr = QR.get()
                    for (o, n) in groups:
                        p, Bp = PS()
                        S.op("pe", MM(p[:, :n], [(wuq[:, k, h * 192:h * 192 + 128], cqn[:, k, o:o + n]) for k in range(3)]),
                             reads=[Bc, Bcqn], writes=[Bp])
                        S.op("act", ACT(qn[:, o:o + n], p[:, :n]), reads=[Bp], writes=[Bqn])
                        p, Bp = PS()
                        S.op("pe", MMS([(p[:64, :n], [(wuq[:, k, h * 192 + 128:h * 192 + 192], cqn[:, k, o:o + n]) for k in range(3)]),
                                        (p[64:128, :n], [(wuqs[:, k, h, :], cqn[:, k, o:o + n]) for k in range(3)])]),
                             reads=[Bc, Bcqn], writes=[Bp])
                        f5, Bf5 = F512.get()
                        S.op("dve", TT_(f5[:64, 0, :n], p[:64, :n], tab2[:, 0, o:o + n], ALU.mult), reads=[Bp, Btab], writes=[Bf5])
                        S.op("dve", TT_(f5[:64, 1, :n], p[64:128, :n], tab2[:, 1, o:o + n], ALU.mult), reads=[Bp, Btab, Bf5], writes=[Bf5])
                        S.op("dve", TT_(qr[:, o:o + n], f5[:64, 0, :n], f5[:64, 1, :n], ALU.add), reads=[Bf5], writes=[Bqr])
                    if smp:
                        sample_attn(h, qn, Bqn, qr, Bqr, mg, Bmg)
                        continue
                    ckT, krT, BckT, BkrT = X["ckT"], X["krT"], X["BckT"], X["BkrT"]
                    nkey = tok0 + ntok
                    kn, Bkn = KN.get()
                    vh, Bvh = VH.get()
                    for (o, n) in splits(nkey, 512):
                        p, Bp = PS()
                        S.op("pe", MM(p[:, :n], [(wukv[:, c, h * 256:h * 256 + 128], ckT[:, c, o:o + n]) for c in range(2)]),
                             reads=[Bc] + BckT[o // 128:(o + n) // 128], writes=[Bp])
                        S.op("act", ACT(kn[:, o:o + n], p[:, :n]), reads=[Bp], writes=[Bkn])
                    for g4 in splits(nkey // 128, 4):
                        p, Bp = PS()
                        S.op("pe", MMS([(p[:, i * 128:(i + 1) * 128],
                                         [(ckT[:, c, (g4[0] + i) * 128:(g4[0] + i + 1) * 128], wukv[:, c, h * 256 + 128:h * 256 + 256]) for c in range(2)])
                                        for i in range(g4[1])]),
                             reads=[Bc] + BckT[g4[0]:g4[0] + g4[1]], writes=[Bp])
                        S.op("act", ACT(vh[:, g4[0]:g4[0] + g4[1], :], p[:, :g4[1] * 128].rearrange("p (t e) -> p t e", e=128)),
                             reads=[Bp], writes=[Bvh])
                    for (o, n) in tiles:
                        qt = (tok0 + o) // 128
                        nk = (qt + 1) * 128
                        kgs = splits(nk, 512)
                        ng = len(kgs)
                        s1, Bs1 = SM1.get()
                        pss = []
                        for gi, (ko, kn_) in enumerate(kgs):
                            p, Bp = PS()
                            pss.append((p, Bp, ko, kn_))
                            S.op("pe", MM(p[:, :kn_], [(qn[:, o:o + n], kn[:, ko:ko + kn_]), (qr[:, o:o + n], krT[:, ko:ko + kn_])]),
                                 reads=[Bqn, Bqr, Bkn] + BkrT[ko // 128:(ko + kn_) // 128], writes=[Bp])
                            if gi == ng - 1:
                                S.op("dve", TT_(p[:, kn_ - 128:kn_], p[:, kn_ - 128:kn_], cm[:], ALU.add), reads=[Bp, Bc], writes=[Bp])
                            S.op("dve", RED(s1[:, gi:gi + 1], p[:, :kn_], ALU.max), reads=[Bp, Bs1], writes=[Bs1])
                        s2_, Bs2 = SM1.get()
                        S.op("dve", RED(s2_[:, 0:1], s1[:, 0:ng], ALU.max), reads=[Bs1], writes=[Bs2])
                        S.op("dve", TS(s2_[:, 1:2], s2_[:, 0:1], -SCALE, None, ALU.mult), reads=[Bs2], writes=[Bs2])
                        pm, Bpm = PM.get()
                        s3, Bs3 = SM1.get()
                        for gi, (p, Bp, ko, kn_) in enumerate(pss):
                            S.op("act", ACT(pm[:, ko:ko + kn_], p[:, :kn_], AF.Exp, scale=SCALE, bias=s2_[:, 1:2], accum_out=s3[:, gi:gi + 1]),
                                 reads=[Bp, Bs2, Bs3], writes=[Bpm, Bs3])
                        S.op("dve", RED(s3[:, 4:5], s3[:, 0:ng], ALU.add), reads=[Bs3], writes=[Bs3])
                        S.op("dve", RECIP(s3[:, 5:6], s3[:, 4:5]), reads=[Bs3], writes=[Bs3])
                        pt_, Bpt = PTT.get()
                        nkt = nk // 128
                        for g8 in splits(nkt, 8):
                            p, Bp = PS()
                            pv = bfv(p)
                            S.op("pe", TR([(pv[:, i * 128:(i + 1) * 128], pm[:, (g8[0] + i) * 128:(g8[0] + i + 1) * 128]) for i in range(g8[1])], idb),
                                 reads=[Bpm, Bc], writes=[Bp])
                            S.op("dve", CP(pt_[:, g8[0]:g8[0] + g8[1], :], pv[:, :g8[1] * 128].rearrange("p (t q) -> p t q", q=128)),
                                 reads=[Bp, Bpt], writes=[Bpt])
                        po, Bpo = PS()
                        S.op("pe", MM(po[:n, :128], [(pt_[:, kt, :n], vh[:, kt, :]) for kt in range(nkt)]), reads=[Bpt, Bvh], writes=[Bpo])
                        on_, Bon = ON.get()
                        S.op("dve", TS(on_[:n, :128], po[:n, :128], s3[:n, 5:6], None, ALU.mult), reads=[Bpo, Bs3], writes=[Bon])
                        p, Bp = PS()
                        pv = bfv(p)
                        S.op("pe", TR([(pv[:, :n], on_[:n, :128])], idb), reads=[Bon, Bc], writes=[Bp])
                        S.op("dve", TT_(moT[:, h, o:o + n], pv[:, :n], mg[:, o:o + n], ALU.mult), reads=[Bp, Bmg], writes=[BmoT[h]])

            if stop == "C":
                return
            with phase():
                WB = RP([128, 8, 512], BF16, 4, "wb")
                QT = RP([128, 2, TBM], BF16, 1, "qT")
                KT = RP([128, 2, TBM], BF16, 1, "kT")
                VT = RP([128, NTB, 512], BF16, 1, "vT")
                GT = RP([128, NTB, 512], BF16, 1, "gT")
                STP = RP([128, 128], BF16, 2, "sT")
                QPP = RP([128, 2, 128], BF16, 2, "qp")
                KD = RP([128, 256], BF16, 2, "kd")
                BNS = RP([128, 6], F32, 2, "bns")
                GG = RP([128, 512], BF16, 2, "gg")
                ON5 = RP([128, 512], BF16, 2, "on5")
                if smp:
                    QF = RP([128, 2, NS], F32, 1, "qf")
                    QMB = RP([128, 2, NS], BF16, 2, "qmb")
                    SIB = RP([128, 2, 512], BF16, 2, "sib")
                    KDMB = RP([NS, 256], BF16, 2, "kdmb")
                    SIN = RP([128, 2, 512], F32, 3, "sin")
                    SOUT = RP([128, 2, 512], F32, 3, "sout")
                    oacc = sb([128, 512], F32); Boacc = Buf("oacc")
                for h in range(4):
                    g = float(np.exp(LG[h]))
                    qT, BqT = QT.get()
                    kT, BkT = KT.get()
                    for (dst, Bdst, col, sc) in ((qT, BqT, C_RQ + h * 256, 1.0), (kT, BkT, C_RK + h * 256, 1.0 / 16)):
                        w, Bw = wload(col, 256, WB)
                        for (o, n) in groups:
                            pa, Bpa = PS()
                            pb_, Bpb = PS()
                            S.op("pe", MM(pa[:, :n], [(w[:, k, 0:128], xnT[:, k, o:o + n]) for k in range(8)]), reads=[Bw, BxnT], writes=[Bpa])
                            S.op("pe", MM(pb_[:, :n], [(w[:, k, 128:256], xnT[:, k, o:o + n]) for k in range(8)]), reads=[Bw, BxnT], writes=[Bpb])
                            xe, Bxe = F512.get()
                            S.op("act", ACT(xe[:, 0, :n], pa[:, :n], AF.Copy, scale=sc), reads=[Bpa], writes=[Bxe])
                            S.op("act", ACT(xe[:, 1, :n], pb_[:, :n], AF.Copy, scale=sc), reads=[Bpb, Bxe], writes=[Bxe])
                            fa, Bfa = F512.get()
                            fb, Bfb = F512.get()
                            csl = tabR[:, 0, o:o + n]
                            snl = tabR[:, 1, o:o + n]
                            S.op("dve", TT_(fa[:, :, :n], xe[:, :, :n], csl.unsqueeze(1).to_broadcast([128, 2, n]), ALU.mult), reads=[Bxe, Btab], writes=[Bfa])
                            S.op("dve", TT_(fb[:, 0, :n], xe[:, 1, :n], snl, ALU.mult), reads=[Bxe, Btab], writes=[Bfb])
                            S.op("dve", TT_(fb[:, 1, :n], xe[:, 0, :n], snl, ALU.mult), reads=[Bxe, Btab, Bfb], writes=[Bfb])
                            S.op("dve", TT_(dst[:, 0, o:o + n], fa[:, 0, :n], fb[:, 0, :n], ALU.subtract), reads=[Bfa, Bfb, Bdst], writes=[Bdst])
                            S.op("dve", TT_(dst[:, 1, o:o + n], fa[:, 1, :n], fb[:, 1, :n], ALU.add), reads=[Bfa, Bfb, Bdst], writes=[Bdst])
                    vT, BvT = VT.get()
                    gT, BgT = GT.get()
                    for (dst, Bdst, col, fn_) in ((vT, BvT, C_RV + h * 512, AF.Copy), (gT, BgT, C_RG + h * 512, AF.Silu)):
                        w, Bw = wload(col, 512, WB)
                        for ti, (o, n) in enumerate(tiles):
                            p, Bp = PS()
                            S.op("pe", MM(p[:n, :], [(xnT[:, k, o:o + n], w[:, k, :]) for k in range(8)]), reads=[Bw, BxnT], writes=[Bp])
                            S.op("act", ACT(dst[:n, ti, :], p[:n, :], fn_), reads=[Bp, Bdst], writes=[Bdst])
                    for ti, (o, n) in enumerate(tiles):
                        first = (not smp) and tok0 == 0 and ti == 0
                        p, Bp = PS()
                        S.op("pe", MM(p[:n, :n], [(kT[:, c, o:o + n], qT[:, c, o:o + n]) for c in range(2)]), reads=[BkT, BqT], writes=[Bp])
                        sT, BsT = STP.get()
                        mask = X["dms"][:n, h, :n] if smp else dm[:n, h, :n]
                        S.op("dve", TT_(sT[:n, :n], p[:n, :n], mask, ALU.mult), reads=[Bp, Bc], writes=[BsT])
                        po, Bpo = PS()
                        kdt, Bkdt = KD.get()
                        p2, Bp2 = PS()
                        pv2 = bfv(p2)
                        S.op("pe", TR([(pv2[:n, c * 128:(c + 1) * 128], kT[:, c, o:o + n]) for c in range(2)], idb), reads=[BkT, Bc], writes=[Bp2])
                        kdsc = X["kds"][:n, h:h + 1] if smp else kd[:n, h:h + 1]
                        S.op("act", ACT(kdt[:n, :], pv2[:n, 0:256], AF.Copy, scale=kdsc), reads=[Bp2, Bc], writes=[Bkdt])
                        if smp:
                            qf, Bqf = QF.get()
                            S.op("dve", TT_(qf[:, :, :n], qT[:, :, o:o + n], X["qds"][:, h, :n].unsqueeze(1).to_broadcast([128, 2, n]), ALU.mult),
                                 reads=[BqT, Bc], writes=[Bqf])
                            S.op("pe", MM(po[:n, :], [(sT[:n, :n], vT[:n, ti, :])]), reads=[BsT, BvT], writes=[Bpo])
                            S.op("act", ACT(oacc[:n, :], po[:n, :]), reads=[Bpo], writes=[Boacc])
                            pc, Bpc = PS()
                            ps_avoid.append(Bpc)
                            for b in range(NSQ):
                                si, Bsi = SIN.get()
                                r0 = (b * 4 + h) * 256
                                S.dma("sp", DMA(si[:], stt[r0:r0 + 256, :].rearrange("(c p) n -> p c n", p=128)), writes=[Bsi])
                                flush_st()
                                sib, Bsib = SIB.get()
                                S.op("act", ACT(sib[:], si[:]), reads=[Bsi], writes=[Bsib])
                                qm, Bqm = QMB.get()
                                S.op("dve", TT_(qm[:, :, :n], qf[:, :, :n], X["bm"][:, b, :n].unsqueeze(1).to_broadcast([128, 2, n]), ALU.mult),
                                     reads=[Bqf, Bc], writes=[Bqm])

                                def cross(e, qm=qm, sib=sib, b=b, pc=pc, n=n):
                                    r = None
                                    for c in range(2):
                                        r = e.matmul(pc[:n, :], lhsT=qm[:, c, :n], rhs=sib[:, c, :],
                                                     start=(b == 0 and c == 0), stop=(b == NSQ - 1 and c == 1))
                                    return r
                                S.op("pe", cross, reads=[Bqm, Bsib], writes=[Bpc])
                                if b == NSQ - 1:
                                    S.op("dve", TT_(oacc[:n, :], oacc[:n, :], pc[:n, :], ALU.add), reads=[Bpc, Boacc], writes=[Boacc])
                                    ps_avoid.remove(Bpc)
                                kdm, Bkdm = KDMB.get()
                                S.op("dve", TS(kdm[:n, :], kdt[:n, :], X["rm"][:n, b:b + 1], None, ALU.mult), reads=[Bkdt, Bc], writes=[Bkdm])
                                pu, Bpu = PS()
                                pu2, Bpu2 = PS()
                                S.op("pe", MM(pu[:, :], [(kdm[:n, 0:128], vT[:n, ti, :])]), reads=[Bkdm, BvT], writes=[Bpu])
                                S.op("pe", MM(pu2[:, :], [(kdm[:n, 128:256], vT[:n, ti, :])]), reads=[Bkdm, BvT], writes=[Bpu2])
                                so, Bso = SOUT.get()
                                S.op("dve", STT(so[:, 0, :], si[:, 0, :], g ** 4, pu[:, :], ALU.mult, ALU.add), reads=[Bsi, Bpu], writes=[Bso])
                                S.op("dve", STT(so[:, 1, :], si[:, 1, :], g ** 4, pu2[:, :], ALU.mult, ALU.add), reads=[Bsi, Bpu2, Bso], writes=[Bso])
                                store(orets[r0:r0 + 256, :].rearrange("(c p) n -> p c n", p=128), so[:], [Bso])
                            osrc, Bosrc = oacc, Boacc
                        else:
                            Sst, Sbf, BSst, BSbf = X["Sst"], X["Sbf"], X["BSst"], X["BSbf"]
                            pairs = [(sT[:n, :n], vT[:n, ti, :])]
                            rd = [BsT, BvT]
                            if not first:
                                qp, Bqp = QPP.get()
                                S.op("dve", TT_(qp[:, :, :n], qT[:, :, o:o + n], qd[:, h, :n].unsqueeze(1).to_broadcast([128, 2, n]), ALU.mult),
                                     reads=[BqT, Bc], writes=[Bqp])
                                pairs += [(qp[:, c, :n], Sbf[:, h, c, :]) for c in range(2)]
                                rd += [Bqp, BSbf[h]]
                            S.op("pe", MM(po[:n, :], pairs), reads=rd, writes=[Bpo])
                            osrc, Bosrc = po, Bpo
                            pu, Bpu = PS()
                            pu2, Bpu2 = PS()
                            S.op("pe", MM(pu[:, :], [(kdt[:n, 0:128], vT[:n, ti, :])]), reads=[Bkdt, BvT], writes=[Bpu])
                            S.op("pe", MM(pu2[:, :], [(kdt[:n, 128:256], vT[:n, ti, :])]), reads=[Bkdt, BvT], writes=[Bpu2])
                            if first:
                                S.op("dve", CP(Sst[:, h, 0, :], pu[:, :]), reads=[Bpu, BSst[h]], writes=[BSst[h]])
                                S.op("dve", CP(Sst[:, h, 1, :], pu2[:, :]), reads=[Bpu2, BSst[h]], writes=[BSst[h]])
                            else:
                                S.op("dve", STT(Sst[:, h, 0, :], Sst[:, h, 0, :], g ** 128, pu[:, :], ALU.mult, ALU.add), reads=[Bpu, BSst[h]], writes=[BSst[h]])
                                S.op("dve", STT(Sst[:, h, 1, :], Sst[:, h, 1, :], g ** 128, pu2[:, :], ALU.mult, ALU.add), reads=[Bpu2, BSst[h]], writes=[BSst[h]])
                            S.op("act", ACT(Sbf[:, h, :, :], Sst[:, h, :, :]), reads=[BSst[h]], writes=[BSbf[h]])
                            if tok0 + o + n == SEQ:
                                S.dma("sp", DMA(oretp[h * 256:(h + 1) * 256, :].rearrange("(c p) n -> p c n", p=128), Sst[:, h, :, :]), reads=[BSst[h]])
                        s1, Bs1 = SM1.get()
                        bs, Bbs = BNS.get()
                        S.op("dve", lambda e, bs=bs, osrc=osrc, n=n: e.bn_stats(out=bs[:n, :], in_=osrc[:n, :]), reads=[Bosrc], writes=[Bbs])
                        S.op("dve", lambda e, bs=bs, s1=s1, n=n: e.bn_aggr(out=s1[:n, 0:2], in_=bs[:n, :]), reads=[Bbs], writes=[Bs1])
                        S.op("act", ACT(s1[:n, 2:3], s1[:n, 1:2], AF.Sqrt, scale=1.0, bias=EPS), reads=[Bs1], writes=[Bs1])
                        S.op("dve", RECIP(s1[:n, 3:4], s1[:n, 2:3]), reads=[Bs1], writes=[Bs1])
                        on_, Bon = ON5.get()
                        S.op("dve", TS(on_[:n, :], osrc[:n, :], s1[:n, 0:1], s1[:n, 3:4], ALU.subtract, ALU.mult), reads=[Bosrc, Bs1], writes=[Bon])
                        gg, Bgg = GG.get()
                        S.op("dve", TT_(gg[:n, :], on_[:n, :], gT[:n, ti, :], ALU.mult), reads=[Bon, BgT], writes=[Bgg])
                        p3, Bp3 = PS()
                        pv3 = bfv(p3)
                        S.op("pe", TR([(pv3[:, c * 128:c * 128 + n], gg[:n, c * 128:(c + 1) * 128]) for c in range(4)], idb), reads=[Bgg, Bc], writes=[Bp3])
                        for c in range(4):
                            S.op("act", ACT(gaT[:, h * 4 + c, o:o + n], pv3[:, c * 128:c * 128 + n], AF.Copy, scale=gnwT[:, h * 4 + c:h * 4 + c + 1]),
                                 reads=[Bp3, Bc, BgaT[h]], writes=[BgaT[h]])

            if stop == "D":
                return
            with phase():
                WD = RP([128, 40, 128], BF16, 2, "wd")
                for cb in range(8):
                    wd, Bwd = WD.get()
                    cs = slice(cb * 128, (cb + 1) * 128)
                    wdma(wd[:, 0:16, :], "w_down_ret", cb * 128, 128, writes=[Bwd])
                    wdma(wd[:, 16:24, :], "w_down_mla", cb * 128, 128, reads=[Bwd], writes=[Bwd])
                    wdma(wd[:, 24:32, :], "w_in", C_GA + cb * 128, 128, reads=[Bwd], writes=[Bwd])
                    wdma(wd[:, 32:40, :], "w_in", C_GB + cb * 128, 128, reads=[Bwd], writes=[Bwd])
                    for (o, n) in groups:
                        p1, B1 = PS(); p2, B2 = PS(); p3, B3 = PS(); p4, B4 = PS()
                        S.op("pe", MM(p1[:, :n], [(wd[:, k, :], gaT[:, k, o:o + n]) for k in range(16)]), reads=[Bwd] + BgaT, writes=[B1])
                        S.op("pe", MM(p2[:, :n], [(wd[:, 16 + k, :], moT[:, k, o:o + n]) for k in range(8)]), reads=[Bwd] + BmoT, writes=[B2])
                        S.op("pe", MM(p3[:, :n], [(wd[:, 24 + k, :], xnT[:, k, o:o + n]) for k in range(8)]), reads=[Bwd, BxnT], writes=[B3])
                        S.op("pe", MM(p4[:, :n], [(wd[:, 32 + k, :], xnT[:, k, o:o + n]) for k in range(8)]), reads=[Bwd, BxnT], writes=[B4])
                        f5, Bf5 = F512.get()
                        S.op("act", ACT(f5[:, 0, :n], p3[:, :n], AF.Sigmoid), reads=[B3], writes=[Bf5])
                        S.op("act", ACT(f5[:, 1, :n], p4[:, :n], AF.Sigmoid), reads=[B4, Bf5], writes=[Bf5])
                        f6, Bf6 = F512.get()
                        S.op("dve", TT_(f6[:, 0, :n], p1[:, :n], f5[:, 0, :n], ALU.mult), reads=[B1, Bf5], writes=[Bf6])
                        S.op("dve", TT_(f6[:, 1, :n], p2[:, :n], f5[:, 1, :n], ALU.mult), reads=[B2, Bf5, Bf6], writes=[Bf6])
                        S.op("dve", TT_(mrT[:, cb, o:o + n], f6[:, 0, :n], f6[:, 1, :n], ALU.add), reads=[Bf6, BmrT], writes=[BmrT])

            if stop == "E":
                return
            with phase():
                WF = RP([128, 18, D], BF16, 1, "wf")
                PLT = RP([128, 256], BF16, 2, "plt")
                XN = RP([128, D], F32, 2, "xn")
                XTT = RP([128, 10, 128], BF16, 2, "xtt")
                wo, Bwo = WF.get()
                wdma(wo[:, 0:8, :], "w_out", 0, D, writes=[Bwo])
                wdma(wo[:, 8:16, :], "w_ple_gate", 0, D, reads=[Bwo], writes=[Bwo])
                wdma(wo[:, 16:18, :], "w_ple_proj", 0, D, reads=[Bwo], writes=[Bwo])
                psrc = psl if smp else ppl
                for (o, n) in tiles:
                    xt, Bxt = XT.get()
                    S.dma("sp", DMA(xt[:n], xsrc[tok0 + o:tok0 + o + n, :]), writes=[Bxt])
                    pl, Bpl = PLT.get()
                    S.dma("pool", DMA(pl[:n, :], psrc[tok0 + o:tok0 + o + n, :]), writes=[Bpl])
                    flush_st()
                    pa, Bpa = PS(); pb_, Bpb = PS()
                    S.op("pe", MM(pa[:n, :], [(mrT[:, k, o:o + n], wo[:, k, 0:512]) for k in range(8)]), reads=[BmrT, Bwo], writes=[Bpa])
                    S.op("pe", MM(pb_[:n, :], [(mrT[:, k, o:o + n], wo[:, k, 512:1024]) for k in range(8)]), reads=[BmrT, Bwo], writes=[Bpb])
                    xn, Bxn = XN.get()
                    S.op("dve", TT_(xn[:n, 0:512], pa[:n, :], xt[:n, 0:512], ALU.add), reads=[Bpa, Bxt], writes=[Bxn])
                    S.op("dve", TT_(xn[:n, 512:1024], pb_[:n, :], xt[:n, 512:1024], ALU.add), reads=[Bpb, Bxt, Bxn], writes=[Bxn])
                    xb, Bxb = XB.get()
                    S.op("act", ACT(xb[:n, :], xn[:n, :]), reads=[Bxn], writes=[Bxb])
                    p, Bp = PS()
                    pv = bfv(p)
                    S.op("pe", TR([(pv[:, c * 128:c * 128 + n], xb[:n, c * 128:(c + 1) * 128]) for c in range(8)], idb), reads=[Bxb, Bc], writes=[Bp])
                    xTt, BxTt = XTT.get()
                    S.op("act", ACT(xTt[:, 0:8, :n], pv.rearrange("p (c t) -> p c t", c=8)[:, :, :n]), reads=[Bp], writes=[BxTt])
                    p, Bp = PS()
                    pv = bfv(p)
                    S.op("pe", TR([(pv[:, c * 128:c * 128 + n], pl[:n, c * 128:(c + 1) * 128]) for c in range(2)], idb), reads=[Bpl, Bc], writes=[Bp])
                    S.op("act", ACT(xTt[:, 8:10, :n], pv[:, 0:256].rearrange("p (c t) -> p c t", c=2)[:, :, :n]), reads=[Bp, BxTt], writes=[BxTt])
                    yt, Byt = XN.get()
                    s1, Bs1 = SM1.get()
                    for hf in range(2):
                        pg, Bpg = PS(); pp_, Bpp = PS()
                        S.op("pe", MM(pg[:n, :], [(xTt[:, k, :n], wo[:, 8 + k, hf * 512:(hf + 1) * 512]) for k in range(8)]), reads=[BxTt, Bwo], writes=[Bpg])
                        S.op("pe", MM(pp_[:n, :], [(xTt[:, 8 + k, :n], wo[:, 16 + k, hf * 512:(hf + 1) * 512]) for k in range(2)]), reads=[BxTt, Bwo], writes=[Bpp])
                        f5, Bf5 = F512.get()
                        S.op("act", ACT(f5[:n, 0, :], pg[:n, :], AF.Sigmoid), reads=[Bpg], writes=[Bf5])
                        S.op("dve", TT_(f5[:n, 1, :], f5[:n, 0, :], pp_[:n, :], ALU.mult), reads=[Bf5, Bpp], writes=[Bf5])
                        S.op("dve", TT_(yt[:n, hf * 512:(hf + 1) * 512], f5[:n, 1, :], xn[:n, hf * 512:(hf + 1) * 512], ALU.add),
                             reads=[Bf5, Bxn, Byt], writes=[Byt])
                    jk, Bjk = JK.get()
                    S.op("act", ACT(jk[:n], yt[:n], AF.Square, accum_out=s1[:n, 0:1]), reads=[Byt], writes=[Bjk, Bs1])
                    S.op("act", ACT(s1[:n, 1:2], s1[:n, 0:1], AF.Sqrt, scale=1.0 / D, bias=EPS), reads=[Bs1], writes=[Bs1])
                    S.op("dve", RECIP(s1[:n, 2:3], s1[:n, 1:2]), reads=[Bs1], writes=[Bs1])
                    yo, Byo = XT.get()
                    S.op("dve", STT(yo[:n, :], yt[:n, :], s1[:n, 2:3], finB[:n, :], ALU.mult, ALU.mult), reads=[Byt, Bs1, Bc], writes=[Byo])
                    ydst = ys if smp else yp
                    store(ydst[tok0 + o:tok0 + o + n, :], yo[:n, :], [Byo])

        def make_sample_attn(X):
            RC = 8
            PK = NPG
            assert PK <= 128 and RC == 8
            NCHK = 128 // RC
            NKC = RC * PK
            qlat = sb([128, 2, NSQ, 32], BF16); Bqlat = Buf("qlat")
            qrs = sb([64, NSQ, 32], BF16); Bqrs = Buf("qrs")
            olT = sb([128, 2, 8, NS], BF16); BolT = Buf("olT")
            mgall = sb([128, 8, NS], BF16); Bmgall = Buf("mgall")
            FST = RP([32, 8], F32, 2, "fst")
            FO = RP([32, 256], F32, 2, "fo")
            CG = RP([128, RC, 256], BF16, 5, "cg")
            RG = RP([128, RC, 64], BF16, 5, "rg")
            CTP = RP([128, NKC], BF16, 2, "ct")
            CTP1 = RP([128, NKC], BF16, 2, "ct1")
            RTP = RP([64, NKC], BF16, 2, "rt")
            PMS = RP([32, NKC], BF16, 2, "pms")
            PTS = RP([128, RC * 32], BF16, 2, "pts")
            OLB = RP([32, 256], BF16, 2, "olb")
            PIX = RP([128, 1], I32, 3, "pix")
            wukT, ckTs, krTs, ckS, Bcks, smk = X["wukT"], X["ckTs"], X["krTs"], X["ckS"], X["Bcks"], X["sm"]
            sci = [0]
            tri = [0]
            BPV = Buf("pv7")

            def SCB():
                k = sci[0] % 4
                sci[0] += 1
                return pbank[k], pbuf[k]

            def TRB():
                k = 4 + tri[0] % 3
                tri[0] += 1
                return pbank[k], pbuf[k]

            def attn(h, qn, Bqn, qr, Bqr, mg, Bmg):
                p, Bp = PS()
                S.op("pe", MMS([(p[:, c * NS:(c + 1) * NS], [(wukT[:, h, c * 128:(c + 1) * 128], qn[:, :NS])]) for c in range(2)]),
                     reads=[Bc, Bqn], writes=[Bp])
                S.op("act", ACT(qlat[:, :, :, h * 4:(h + 1) * 4], p[:, :2 * NS].rearrange("p (c b t) -> p c b t", c=2, t=4)),
                     reads=[Bp, Bqlat], writes=[Bqlat])
                S.op("dve", CP(qrs[:, :, h * 4:(h + 1) * 4], qr[:, :NS].rearrange("p (b t) -> p b t", t=4)), reads=[Bqr, Bqrs], writes=[Bqrs])
                S.op("dve", CP(mgall[:, h, :], mg[:, :NS]), reads=[Bmg, Bmgall], writes=[Bmgall])
                if h < 7:
                    return
                items = []
                for b in range(NSQ):
                    items.append({"b": b, "ck": "new", "first": True, "last": False})
                    for ci in range(NCHK):
                        items.append({"b": b, "ck": "old", "ci": ci, "first": False, "last": ci == NCHK - 1})
                for k_, it_ in enumerate(items):
                    it_["conv"] = (k_ % 3 == 0)
                seqst = {}

                def stage_G(it):
                    b = it["b"]
                    if it.get("conv"):
                        conv_emit_some(1)
                    if it["first"]:
                        pix, Bpix = PIX.get()
                        S.dma("sp", DMA(pix[:PK, 0:1], ptab[0:1, b * NPG:(b + 1) * NPG].rearrange("o (p q) -> (o p) q", q=1)), writes=[Bpix])
                        S.op("dve", TS(pix[:PK, 0:1], pix[:PK, 0:1], NCHK, None, ALU.mult), reads=[Bpix], writes=[Bpix])
                        fm, Bfm = FST.get()
                        oa, Boa = FO.get()
                        seqst[b] = dict(pix=pix, Bpix=Bpix, fm=fm, Bfm=Bfm, oa=oa, Boa=Boa)
                        return
                    stq = seqst[b]
                    pix, Bpix = stq["pix"], stq["Bpix"]
                    cg, Bcg = CG.get()
                    rg_, Brg = RG.get()
                    r0 = it["ci"] * RC
                    S.dma("pool", lambda e, cg=cg, pix=pix, r0=r0: e.indirect_dma_start(
                        out=cg[:PK, :, :].rearrange("p r c -> p (r c)"), out_offset=None, in_=cck,
                        in_offset=bass.IndirectOffsetOnAxis(ap=pix[:PK, 0:1], axis=0), element_offset=r0 * 256), reads=[Bpix, Bcg], writes=[Bcg])
                    S.dma("pool", lambda e, rg_=rg_, pix=pix, r0=r0: e.indirect_dma_start(
                        out=rg_[:PK, :, :].rearrange("p r c -> p (r c)"), out_offset=None, in_=ckr,
                        in_offset=bass.IndirectOffsetOnAxis(ap=pix[:PK, 0:1], axis=0), element_offset=r0 * 64), reads=[Bpix, Brg], writes=[Brg])
                    it.update(cg=cg, Bcg=Bcg, rg=rg_, Brg=Brg)

                def stage_T(it):
                    b = it["b"]
                    if it["ck"] == "new":
                        nk = NS
                        CT_c = [ckTs[:, 0, :], ckTs[:, 1, :]]
                        RT_ = krTs[:, :]
                        rdk = [Bcks]
                        it.update(vals=[(ckS[:NS, :], NS)], rdv=[Bcks], kts=[(0, NS)])
                    else:
                        nk = NKC
                        cg, Bcg, rg_, Brg = it["cg"], it["Bcg"], it["rg"], it["Brg"]
                        ct, Bct = CTP.get()
                        ct1, Bct1 = CTP1.get()
                        rt, Brt = RTP.get()
                        for c in range(2):
                            p, Bp = TRB()
                            pv = bfv(p)
                            S.op("pe", TR([(pv[:, r * PK:(r + 1) * PK], cg[:PK, r, c * 128:(c + 1) * 128]) for r in range(RC)], idb),
                                 reads=[Bcg, Bc], writes=[Bp])
                            if c == 0:
                                S.op("act", ACT(ct[:, :], pv[:, :NKC]), reads=[Bp], writes=[Bct])
                            else:
                                S.op("dve", CP(ct1[:, :], pv[:, :NKC]), reads=[Bp], writes=[Bct1])
                        p, Bp = TRB()
                        pv = bfv(p)
                        S.op("pe", TR([(pv[:64, r * PK:(r + 1) * PK], rg_[:PK, r, :]) for r in range(RC)], idb), reads=[Brg, Bc], writes=[Bp])
                        S.op("act", ACT(rt[:, :], pv[:64, :NKC]), reads=[Bp], writes=[Brt])
                        CT_c = [ct[:, :], ct1[:, :]]
                        RT_ = rt[:, :]
                        rdk = [Bct, Bct1, Brt]
                        it.update(vals=[(cg[:PK, r, :], PK) for r in range(RC)], rdv=[Bcg], kts=[(r * PK, PK) for r in range(RC)])
                    kgs = splits(nk, 512)
                    s1, Bs1 = SM1.get()
                    pss = []
                    for gi, (ko, kn_) in enumerate(kgs):
                        p, Bp = SCB()
                        pss.append((p, Bp, ko, kn_))
                        S.op("pe", MM(p[:32, :kn_], [(qlat[:, 0, b, :], CT_c[0][:, ko:ko + kn_]), (qlat[:, 1, b, :], CT_c[1][:, ko:ko + kn_]),
                                                      (qrs[:, b, :], RT_[:, ko:ko + kn_])]), reads=[Bqlat, Bqrs] + rdk, writes=[Bp])
                        if it["ck"] == "new":
                            S.op("dve", TT_(p[:32, :kn_], p[:32, :kn_], smk[:, b, :], ALU.add), reads=[Bp, Bc], writes=[Bp])
                        S.op("dve", RED(s1[:32, gi:gi + 1], p[:32, :kn_], ALU.max), reads=[Bp, Bs1], writes=[Bs1])
                    it.update(pss=pss, s1=s1, Bs1=Bs1, nk=nk)

                def stage_X(it):
                    b = it["b"]
                    stq = seqst[b]
                    fm, Bfm, oacc_, Boa = stq["fm"], stq["Bfm"], stq["oa"], stq["Boa"]
                    if it["first"]:
                        S.op("dve", MSET(fm[:, 0:1], -1.0e4), reads=[Bfm], writes=[Bfm])
                        S.op("dve", MSET(fm[:, 1:2], 0.0), reads=[Bfm], writes=[Bfm])
                        S.op("dve", MSET(oacc_[:], 0.0), reads=[Boa], writes=[Boa])
                    pss, s1, Bs1, nk = it["pss"], it["s1"], it["Bs1"], it["nk"]
                    ng = len(pss)
                    s2_, Bs2 = SM1.get()
                    S.op("dve", RED(s2_[:32, 0:1], s1[:32, 0:ng], ALU.max), reads=[Bs1], writes=[Bs2])
                    S.op("dve", TT_(s2_[:32, 1:2], s2_[:32, 0:1], fm[:, 0:1], ALU.max), reads=[Bs2, Bfm], writes=[Bs2])
                    S.op("dve", TS(s2_[:32, 2:3], s2_[:32, 1:2], -SCALE, None, ALU.mult), reads=[Bs2], writes=[Bs2])
                    S.op("act", ACT(s2_[:32, 3:4], fm[:, 0:1], AF.Exp, scale=SCALE, bias=s2_[:32, 2:3]), reads=[Bs2, Bfm], writes=[Bs2])
                    S.op("dve", CP(fm[:, 0:1], s2_[:32, 1:2]), reads=[Bs2, Bfm], writes=[Bfm])
                    pm, Bpm = PMS.get()
                    s3, Bs3 = SM1.get()
                    for gi, (p, Bp, ko, kn_) in enumerate(pss):
                        S.op("act", ACT(pm[:, ko:ko + kn_], p[:32, :kn_], AF.Exp, scale=SCALE, bias=s2_[:32, 2:3], accum_out=s3[:32, gi:gi + 1]),
                             reads=[Bp, Bs2, Bs3, Bpm], writes=[Bpm, Bs3])
                    S.op("dve", RED(s3[:32, 4:5], s3[:32, 0:ng], ALU.add), reads=[Bs3], writes=[Bs3])
                    S.op("dve", STT(fm[:, 1:2], fm[:, 1:2], s2_[:32, 3:4], s3[:32, 4:5], ALU.mult, ALU.add), reads=[Bfm, Bs2, Bs3], writes=[Bfm])
                    it.update(pm=pm, Bpm=Bpm, s2_=s2_, Bs2=Bs2)

                def stage_XB(it):
                    pm, Bpm = it["pm"], it["Bpm"]
                    p7 = pbank[7]
                    pv = bfv(p7)
                    pts, Bpts = PTS.get()
                    kts = it["kts"]
                    S.op("pe", TR([(pv[:kn_, i * 32:(i + 1) * 32], pm[:, ko:ko + kn_]) for i, (ko, kn_) in enumerate(kts)], idb), reads=[Bpm, Bc], writes=[pbuf[7]])
                    kmax = max(kn_ for (_, kn_) in kts)
                    S.op("dve", CP(pts[:kmax, :len(kts) * 32], pv[:kmax, :len(kts) * 32]), reads=[pbuf[7]], writes=[Bpts])
                    it.update(pts=pts, Bpts=Bpts)

                def stage_X2(it):
                    b = it["b"]
                    stq = seqst[b]
                    fm, Bfm, oacc_, Boa = stq["fm"], stq["Bfm"], stq["oa"], stq["Boa"]
                    pts, Bpts, s2_, Bs2 = it["pts"], it["Bpts"], it["s2_"], it["Bs2"]
                    p7 = pbank[7]
                    S.op("pe", MM(p7[:32, 256:512], [(pts[:vn, i * 32:(i + 1) * 32], vap) for i, (vap, vn) in enumerate(it["vals"])]),
                         reads=[Bpts] + it["rdv"], writes=[BPV])
                    S.op("dve", STT(oacc_[:, :], oacc_[:, :], s2_[:32, 3:4], p7[:32, 256:512], ALU.mult, ALU.add), reads=[Boa, Bs2, BPV], writes=[Boa])
                    if it["last"]:
                        S.op("dve", RECIP(fm[:, 2:3], fm[:, 1:2]), reads=[Bfm], writes=[Bfm])
                        ob, Bob = OLB.get()
                        S.op("dve", TS(ob[:, :], oacc_[:, :], fm[:, 2:3], None, ALU.mult), reads=[Boa, Bfm], writes=[Bob])
                        p, Bp = TRB()
                        pv2 = bfv(p)
                        S.op("pe", TR([(pv2[:, c * 32:(c + 1) * 32], ob[:, c * 128:(c + 1) * 128]) for c in range(2)], idb), reads=[Bob, Bc], writes=[Bp])
                        S.op("act", ACT(olT[:, :, :, b * 4:(b + 1) * 4], pv2[:, 0:64].rearrange("p (c h t) -> p c h t", c=2, t=4)), reads=[Bp, BolT], writes=[BolT])

                n_it = len(items)
                for step in range(n_it + 3):
                    if step < n_it:
                        stage_G(items[step])
                    if 0 <= step - 3 < n_it:
                        stage_XB(items[step - 3])
                    if 0 <= step - 1 < n_it:
                        stage_T(items[step - 1])
                    if 0 <= step - 2 < n_it:
                        stage_X(items[step - 2])
                    if 0 <= step - 3 < n_it:
                        stage_X2(items[step - 3])
                if BPV.lw is not None:
                    pbuf[7].rd.extend([BPV.lw] + BPV.rd)
                for hh in range(8):
                    p, Bp = PS()
                    S.op("pe", MM(p[:, :NS], [(wukv[:, c, hh * 256 + 128:hh * 256 + 256], olT[:, c, hh, :]) for c in range(2)]), reads=[Bc, BolT], writes=[Bp])
                    S.op("dve", TT_(moT[:, hh, :NS], p[:, :NS], mgall[:, hh, :], ALU.mult), reads=[Bp, Bmgall], writes=[BmoT[hh]])
            return attn

        with phase():
            X = {}
            X["dms"] = cload([NS, 4, NS], F32, k_dms); X["qds"] = cload([128, 4, NS], F32, k_qds)
            X["kds"] = cload([NS, 4], F32, k_kds)
            X["bm"] = cload([128, NSQ, NS], F32, k_bm); X["rm"] = cload([NS, NSQ], F32, k_rm)
            X["sm"] = cload([32, NSQ, NS], F32, k_sm)
            X["io"] = cload([128, 1], I32, k_io)
            X["ckTs"] = sb([128, 2, NS], BF16); X["krTs"] = sb([64, NS], BF16); X["ckS"] = sb([NS, 256], BF16)
            X["Bcks"] = Buf("cks")
            wukT = sb([128, 8, 256], BF16)
            for h in range(8):
                p, Bp = PS()
                pv = bfv(p)
                S.op("pe", TR([(pv[:, c * 128:(c + 1) * 128], wukv[:, c, h * 256:h * 256 + 128]) for c in range(2)], idb),
                     reads=[Bc], writes=[Bp])
                S.op("act", ACT(wukT[:, h, :], pv[:, 0:256]), reads=[Bp, Bc], writes=[Bc])
            X["wukT"] = wukT
            block("s", 0, NS, X)
        with phase():
            X = {}
            X["ckT"] = sb([128, 2, SEQ], BF16); X["BckT"] = [Buf() for _ in range(SEQ // 128)]
            X["krT"] = sb([64, SEQ], BF16); X["BkrT"] = [Buf() for _ in range(SEQ // 128)]
            X["Sst"] = sb([128, 4, 2, 512], F32); X["BSst"] = [Buf(f"S{h}") for h in range(4)]
            X["Sbf"] = sb([128, 4, 2, 512], BF16); X["BSbf"] = [Buf(f"Sb{h}") for h in range(4)]
            for t0 in range(0, SEQ, TB):
                block("p", t0, TB, X)
        S.emit()
    return nc


def host_consts(SEQ, NSQ, NPG):
    NS = NSQ * 4
    TT = SEQ + NS
    past = NPG * 128
    pos = np.concatenate([np.arange(SEQ, dtype=np.float32),
                          np.tile(past + np.arange(4, dtype=np.float32), NSQ)]).astype(np.float32)
    c = {}
    invR = (10000.0 ** (-np.arange(128, dtype=np.float32) / 128)).astype(np.float32)
    angR = (pos[None, :] * invR[:, None]).astype(np.float32)
    c["k_cR"] = np.cos(angR).astype(np.float32); c["k_sR"] = np.sin(angR).astype(np.float32)
    invM = (10000.0 ** (-np.arange(32, dtype=np.float32) / 32)).astype(np.float32)
    angM = (pos[None, :] * invM[:, None]).astype(np.float32)
    cm_, sm_ = np.cos(angM).astype(np.float32), np.sin(angM).astype(np.float32)
    c["k_c2"] = np.concatenate([cm_, cm_], 0); c["k_s2"] = np.concatenate([-sm_, sm_], 0)
    c["k_cM"] = np.ascontiguousarray(cm_.T); c["k_sM"] = np.ascontiguousarray(sm_.T)
    c["k_id"] = np.eye(128, dtype=np.float32)
    lg = np.log1p(-np.exp2(-5.0 - np.arange(4, dtype=np.float64)))
    i = np.arange(128)
    diff = i[None, :] - i[:, None]
    dm = np.zeros((128, 4, 128), np.float32)
    for h in range(4):
        dm[:, h, :] = np.where(diff >= 0, np.exp(np.maximum(diff, 0) * lg[h]), 0.0)
    c["k_dm"] = dm
    tt = np.arange(NS) % 4
    bb = np.arange(NS) // 4
    dms = np.zeros((NS, 4, NS), np.float32)
    for h in range(4):
        d = tt[None, :] - tt[:, None]
        dms[:, h, :] = np.where((bb[None, :] == bb[:, None]) & (d >= 0), np.exp(np.maximum(d, 0) * lg[h]), 0.0)
    c["k_dms"] = dms
    c["k_qd"] = np.broadcast_to(np.exp((i[None, :] + 1.0) * lg[:, None])[None], (128, 4, 128)).astype(np.float32).copy()
    c["k_qds"] = np.broadcast_to(np.exp((tt[None, :] + 1.0) * lg[:, None])[None], (128, 4, NS)).astype(np.float32).copy()
    c["k_kd"] = np.exp((127.0 - i)[:, None] * lg[None, :]).astype(np.float32)
    c["k_kds"] = np.exp((3.0 - tt)[:, None] * lg[None, :]).astype(np.float32)
    bmk = (bb[None, :] == np.arange(NSQ)[:, None]).astype(np.float32)
    c["k_bm"] = np.broadcast_to(bmk[None], (128, NSQ, NS)).copy()
    c["k_rm"] = np.ascontiguousarray(bmk.T)
    c["k_cm"] = np.where(i[None, :] <= i[:, None], 0.0, NEG).astype(np.float32)
    tq = np.arange(32) % 4
    smk = np.full((32, NSQ, NS), NEG, np.float32)
    for b in range(NSQ):
        ok = (bb[None, :] == b) & (tt[None, :] <= tq[:, None])
        smk[:, b, :] = np.where(ok, 0.0, NEG)
    c["k_sm"] = smk
    c["k_io"] = np.arange(128, dtype=np.int32)[:, None]
    return c


_CACHE = {}


def run(inputs, SEQ, NSQ, NPG, NPOOL, TB, ncores):
    key = (SEQ, NSQ, NPG, NPOOL, TB)
    if key not in _CACHE:
        _CACHE[key] = build(*key)
    nc = _CACHE[key]
    f = lambda a: np.ascontiguousarray(np.asarray(a, dtype=np.float32))
    consts = host_consts(SEQ, NSQ, NPG)
    cck = f(inputs["cache_ckv"][0]).reshape(NPOOL * 16, 8 * 256)
    ckr = f(inputs["cache_krope"][0]).reshape(NPOOL * 16, 8 * 64)
    shared = dict(
        cck=cck, ckr=ckr,
        ln_w=f(inputs["ln_w"][0]), w_in=f(inputs["w_in"][0]), q_norm_w=f(inputs["q_norm_w"][0]), w_uq=f(inputs["w_uq"][0]),
        kv_norm_w=f(inputs["kv_norm_w"][0]).reshape(1, 256), w_ukv=f(inputs["w_ukv"][0]), ret_gn_w=f(inputs["ret_gn_w"][0]),
        w_down_ret=f(inputs["w_down_ret"][0]), w_down_mla=f(inputs["w_down_mla"][0]), w_out=f(inputs["w_out"][0]),
        w_ple_gate=f(inputs["w_ple_gate"][0]), w_ple_proj=f(inputs["w_ple_proj"][0]), fin_w=f(inputs["final_norm_w"]).reshape(1, D),
        **consts)
    in_maps = []
    NS = NSQ * 4
    for c in range(ncores):
        m = dict(shared)
        m["xp"] = f(inputs["x_prompt"][c])
        m["xs"] = f(inputs["x_sample"][c * NSQ:(c + 1) * NSQ]).reshape(NS, D)
        m["stt"] = f(inputs["state_ret"][0, c * NSQ:(c + 1) * NSQ]).reshape(NSQ * 4 * 256, 512)
        m["ptab"] = np.ascontiguousarray(np.asarray(inputs["page_table"][c * NSQ:(c + 1) * NSQ], dtype=np.int32)).reshape(1, NSQ * NPG)
        m["ppl"] = f(inputs["p_prompt"][0, c])
        m["psl"] = f(inputs["p_sample"][0, c * NSQ:(c + 1) * NSQ]).reshape(NS, 256)
        in_maps.append(m)
    res = run_bass_kernel_spmd(nc, in_maps, core_ids=list(range(ncores)))
    R = res.results
    cat = lambda k: np.stack([R[c][k] for c in range(ncores)], 0)
    y_p = cat("yp")
    y_s = cat("ys").reshape(ncores * NSQ, 4, D)
    ckv_p = cat("ockp")[None]
    kr_p = cat("okrp")[None]
    ret_p = cat("oretp").reshape(1, ncores, 4, 256, 512)
    ckv_s = cat("ocks").reshape(1, ncores * NSQ, 4, 256)
    kr_s = cat("okrs").reshape(1, ncores * NSQ, 4, 64)
    ret_s = cat("orets").reshape(1, ncores * NSQ, 4, 256, 512)
    return (y_p, y_s, ckv_p, kr_p, ret_p, ckv_s, kr_s, ret_s)


def kernel(**inputs):
    B, SEQ, _ = inputs["x_prompt"].shape
    DB = inputs["x_sample"].shape[0]
    NPG = inputs["page_table"].shape[1]
    NPOOL = inputs["cache_ckv"].shape[1]
    ncores = 8
    assert B == ncores
    return run(inputs, SEQ, DB // ncores, NPG, NPOOL, min(512, SEQ), ncores)
```

```python
import contextlib
import os
import numpy as np
import concourse.bass as bass
import concourse.mybir as mybir
from concourse.bass_utils import run_bass_kernel_spmd

F32 = mybir.dt.float32
BF16 = mybir.dt.bfloat16
I32 = mybir.dt.int32
AF = mybir.ActivationFunctionType
ALU = mybir.AluOpType
AX = mybir.AxisListType

D = 1024
EPS = 1e-6
SCALE = 192 ** -0.5
NEG = -30000.0
C_RQ, C_RK, C_RV, C_RG, C_CQ, C_CKV, C_KR, C_MG, C_GA, C_GB = 0, 1024, 2048, 4096, 6144, 6528, 6784, 6848, 7872, 8896
INW = 9920


class Buf:
    __slots__ = ("name", "lw", "rd")

    def __init__(self, name=""):
        self.name = name
        self.lw = None
        self.rd = []


class Op:
    __slots__ = ("eng", "fn", "kind", "eidx", "waits", "signal", "sigval", "sem", "semval", "prewait", "waitall")


ENGS = ("pe", "act", "dve", "pool", "sp")
NQ = 8


class Sched:
    def __init__(self, nc):
        self.nc = nc
        self.eops = {e: [] for e in ENGS}
        self.ndma = {e: 0 for e in ENGS}
        self.seen = {e: {p: -1 for p in ENGS} for e in ENGS}
        self.seen_dma = {e: set() for e in ENGS}

    def op(self, eng, fn, reads=(), writes=(), kind="c"):
        o = Op()
        o.eng, o.fn, o.kind = eng, fn, kind
        o.waits, o.signal, o.sigval, o.sem, o.semval, o.prewait = [], False, 0, None, 0, None
        o.waitall = None
        o.eidx = len(self.eops[eng])
        self.eops[eng].append(o)
        deps = []
        for b in reads:
            if b.lw is not None:
                deps.append(b.lw)
        for b in writes:
            if b.lw is not None:
                deps.append(b.lw)
            deps.extend(b.rd)
        for b in writes:
            b.lw = o
            b.rd = []
        for b in reads:
            if b.lw is not o:
                b.rd.append(o)
        seen = self.seen[eng]
        for d in deps:
            if d is o:
                continue
            if d.kind == "d":
                if id(d) in self.seen_dma[eng]:
                    continue
                self.seen_dma[eng].add(id(d))
                o.waits.append(d)
            else:
                if d.eng == eng and eng == "pe":
                    continue
                if d.eidx <= seen[d.eng]:
                    continue
                seen[d.eng] = d.eidx
                d.signal = True
                o.waits.append(d)
        if kind == "d":
            j = self.ndma[eng]
            self.ndma[eng] = j + 1
            o.sem = j % NQ
            o.semval = 16 * (j // NQ + 1)
            if j >= NQ:
                o.prewait = (j % NQ, 16 * (j // NQ))
        return o

    def dma(self, eng, fn, reads=(), writes=()):
        return self.op(eng, fn, reads, writes, kind="d")

    def barrier(self, tiny1, tiny2):
        if not hasattr(self, "bar"):
            self.bar = {e: Buf("bar" + e) for e in ENGS}
            self.bar2 = {e: Buf("bar2" + e) for e in ENGS}
        for e in ENGS:
            nd = self.ndma[e]
            o = self.op(e, tiny1[e], reads=(), writes=(self.bar[e],), kind=("d" if e == "sp" else "c"))
            o.waitall = nd
        for e in ENGS:
            self.op(e, tiny2[e], reads=[self.bar[x] for x in ENGS], writes=(self.bar2[e],), kind=("d" if e == "sp" else "c"))

    def emit(self):
        nc = self.nc
        with contextlib.ExitStack() as st:
            esem = {e: st.enter_context(nc.semaphore(f"s_{e}")) for e in ENGS}
            dsem = {e: [st.enter_context(nc.semaphore(f"d_{e}{i}")) for i in range(NQ)]
                    for e in ENGS if self.ndma[e] > 0}
            for e in ENGS:
                c = 0
                for o in self.eops[e]:
                    if o.kind == "c" and o.signal:
                        c += 1
                        o.sigval = c
            block = st.enter_context(nc.Block())

            def run(e, eng):
                for o in self.eops[e]:
                    if o.waitall is not None:
                        for j in range(max(0, o.waitall - NQ), o.waitall):
                            eng.wait_ge(dsem[e][j % NQ], 16 * (j // NQ + 1))
                    if o.prewait is not None:
                        eng.wait_ge(dsem[e][o.prewait[0]], o.prewait[1])
                    for d in o.waits:
                        if d.kind == "d":
                            eng.wait_ge(dsem[d.eng][d.sem], d.semval)
                        else:
                            eng.wait_ge(esem[d.eng], d.sigval)
                    inst = o.fn(eng)
                    if o.kind == "d":
                        inst.then_inc(dsem[e][o.sem], 16)
                    elif o.signal:
                        inst.then_inc(esem[e], 1)
                n = self.ndma[e]
                for j in range(max(0, n - NQ), n):
                    eng.wait_ge(dsem[e][j % NQ], 16 * (j // NQ + 1))

            @block.tensor
            def _(eng):
                run("pe", eng)

            @block.scalar
            def _(eng):
                run("act", eng)

            @block.vector
            def _(eng):
                run("dve", eng)

            @block.gpsimd
            def _(eng):
                run("pool", eng)

            @block.sync
            def _(eng):
                run("sp", eng)


def splits(n, step):
    return [(o, min(step, n - o)) for o in range(0, n, step)]


def build(SEQ, NSQ, NPG, NPOOL, TB):
    NS = NSQ * 4
    TT = SEQ + NS
    CH = min(8, NPG)
    NCH = NPG // CH
    assert NPG % CH == 0 and SEQ % TB == 0 and TB % 128 == 0
    TBM = max(TB, NS)
    NTB = (TBM + 127) // 128
    nc = bass.Bass("TRN2", target_bir_lowering=False)

    def din(name, shape, dt=F32):
        return nc.dram_tensor(name, shape, dt, kind="ExternalInput").ap()

    def dout(name, shape):
        return nc.dram_tensor(name, shape, F32, kind="ExternalOutput").ap()

    xp = din("xp", [SEQ, D]); xs = din("xs", [NS, D])
    cck = din("cck", [NPOOL * 16, 8 * 256]); ckr = din("ckr", [NPOOL * 16, 8 * 64])
    stt = din("stt", [NSQ * 4 * 256, 512])
    ptab = din("ptab", [1, NSQ * NPG], I32)
    ppl = din("ppl", [SEQ, 256]); psl = din("psl", [NS, 256])
    ln_w = din("ln_w", [D]); w_in = din("w_in", [D, INW]); q_norm_w = din("q_norm_w", [384])
    w_uq = din("w_uq", [384, 1536]); kv_norm_w = din("kv_norm_w", [1, 256]); w_ukv = din("w_ukv", [256, 2048])
    ret_gn_w = din("ret_gn_w", [2048]); w_down_ret = din("w_down_ret", [2048, D])
    w_down_mla = din("w_down_mla", [D, D]); w_out = din("w_out", [D, D])
    w_ple_gate = din("w_ple_gate", [D, D]); w_ple_proj = din("w_ple_proj", [256, D])
    fin_w = din("fin_w", [1, D])
    k_cR = din("k_cR", [128, TT]); k_sR = din("k_sR", [128, TT])
    k_c2 = din("k_c2", [64, TT]); k_s2 = din("k_s2", [64, TT])
    k_cM = din("k_cM", [TT, 32]); k_sM = din("k_sM", [TT, 32])
    k_id = din("k_id", [128, 128])
    k_dm = din("k_dm", [128, 4, 128]); k_dms = din("k_dms", [NS, 4, NS])
    k_qd = din("k_qd", [128, 4, 128]); k_qds = din("k_qds", [128, 4, NS])
    k_kd = din("k_kd", [128, 4]); k_kds = din("k_kds", [NS, 4])
    k_bm = din("k_bm", [128, NSQ, NS]); k_rm = din("k_rm", [NS, NSQ])
    k_cm = din("k_cm", [128, 128]); k_sm = din("k_sm", [32, NSQ, NS])
    k_io = din("k_io", [128, 1], I32)

    yp = dout("yp", [SEQ, D]); ys = dout("ys", [NS, D])
    ockp = dout("ockp", [SEQ, 256]); okrp = dout("okrp", [SEQ, 64])
    oretp = dout("oretp", [4 * 256, 512])
    ocks = dout("ocks", [NS, 256]); okrs = dout("okrs", [NS, 64])
    orets = dout("orets", [NSQ * 4 * 256, 512])

    S = Sched(nc)
    LG = [float(np.log1p(-np.exp2(-5.0 - h))) for h in range(4)]
    WSRC = {"w_in": w_in, "w_down_ret": w_down_ret, "w_down_mla": w_down_mla, "w_out": w_out,
            "w_ple_gate": w_ple_gate, "w_ple_proj": w_ple_proj}
    WSCR = {k: nc.dram_tensor(k + "_b", list(v.shape), BF16, kind="Internal").ap() for k, v in WSRC.items()}
    conv_done = {}
    conv_pending = {}

    def conv_register(name, col0, ncols):
        conv_pending[(name, col0, ncols)] = None

    def conv_emit(key):
        if key in conv_done:
            return conv_done[key]
        conv_pending.pop(key, None)
        name, col0, ncols = key
        B = Buf(str(key))
        S.dma("pool", lambda e: e.dma_start(out=WSCR[name][:, col0:col0 + ncols], in_=WSRC[name][:, col0:col0 + ncols]), writes=[B])
        conv_done[key] = B
        return B

    def conv_emit_some(n):
        for _ in range(n):
            if conv_pending:
                conv_emit(next(iter(conv_pending)))

    conv_register("w_in", C_CQ, 384)
    conv_register("w_in", C_CKV, 320)
    for h in range(8):
        conv_register("w_in", C_MG + h * 128, 128)
    for h in range(4):
        conv_register("w_in", C_RQ + h * 256, 256)
        conv_register("w_in", C_RK + h * 256, 256)
        conv_register("w_in", C_RV + h * 512, 512)
        conv_register("w_in", C_RG + h * 512, 512)
    for cb in range(8):
        conv_register("w_down_ret", cb * 128, 128)
        conv_register("w_down_mla", cb * 128, 128)
        conv_register("w_in", C_GA + cb * 128, 128)
        conv_register("w_in", C_GB + cb * 128, 128)
    conv_register("w_out", 0, D)
    conv_register("w_ple_gate", 0, D)
    conv_register("w_ple_proj", 0, D)

    def wdma(dst, name, col0, ncols, reads=(), writes=()):
        B = conv_emit((name, col0, ncols))
        S.dma("sp", lambda e: e.dma_start(out=dst, in_=WSCR[name][:, col0:col0 + ncols].rearrange("(c p) n -> p c n", p=128)),
              reads=[B] + list(reads), writes=list(writes))

    with contextlib.ExitStack() as st:
        cnt = [0]
        stk = [st]

        def sb(shape, dt, name=None):
            cnt[0] += 1
            return stk[-1].enter_context(nc.sbuf_tensor(f"{name or 't'}_{cnt[0]}", shape, dt))

        class RP:
            def __init__(self, shape, dt, n, name):
                self.t = [sb(shape, dt, f"{name}{i}") for i in range(n)]
                self.b = [Buf(f"{name}{i}") for i in range(n)]
                self.i = 0

            def get(self):
                k = self.i % len(self.t)
                self.i += 1
                return self.t[k], self.b[k]

        pbank = [st.enter_context(nc.psum_tensor(f"pb{i}", [128, 512], F32)) for i in range(8)]
        pbuf = [Buf(f"pb{i}") for i in range(8)]
        pidx = [0]

        ps_avoid = []

        def PS():
            k = pidx[0] % 8
            pidx[0] += 1
            while pbuf[k] in ps_avoid:
                k = pidx[0] % 8
                pidx[0] += 1
            return pbank[k], pbuf[k]

        def bfv(p):
            return p[:].bitcast(BF16)

        def MM(out, pairs):
            def f(e):
                n = len(pairs)
                r = None
                for i, (l, rh) in enumerate(pairs):
                    r = e.matmul(out, lhsT=l, rhs=rh, start=(i == 0), stop=(i == n - 1))
                return r
            return f

        def MMS(groups):
            def f(e):
                r = None
                for out, pairs in groups:
                    n = len(pairs)
                    for i, (l, rh) in enumerate(pairs):
                        r = e.matmul(out, lhsT=l, rhs=rh, start=(i == 0), stop=(i == n - 1))
                return r
            return f

        def TR(items, idn):
            def f(e):
                r = None
                for o, i in items:
                    k = i.shape[0]
                    r = e.transpose(o, i, idn[:k, :k])
                return r
            return f

        def ACT(out, in_, func=AF.Copy, **kw):
            return lambda e: e.activation(out=out, in_=in_, func=func, **kw)

        def TT_(out, a, b, op):
            return lambda e: e.tensor_tensor(out=out, in0=a, in1=b, op=op)

        def TS(out, a, s1, s2, op0, op1=None):
            if op1 is None:
                return lambda e: e.tensor_scalar(out=out, in0=a, scalar1=s1, scalar2=None, op0=op0)
            return lambda e: e.tensor_scalar(out=out, in0=a, scalar1=s1, scalar2=s2, op0=op0, op1=op1)

        def STT(out, a, s, b, op0, op1):
            return lambda e: e.scalar_tensor_tensor(out=out, in0=a, scalar=s, in1=b, op0=op0, op1=op1)

        def DMA(out, in_, **kw):
            return lambda e: e.dma_start(out=out, in_=in_, **kw)

        def CP(out, in_):
            return lambda e: e.tensor_copy(out=out, in_=in_)

        pending_st = []

        def store(out, in_, reads):
            pending_st.append((DMA(out, in_), list(reads)))

        def flush_st():
            for fn, rd in pending_st:
                S.dma("sp", fn, reads=rd)
            del pending_st[:]

        def RECIP(out, in_):
            return lambda e: e.reciprocal(out=out, in_=in_)

        def RED(out, in_, op):
            return lambda e: e.tensor_reduce(out=out, in_=in_, axis=AX.X, op=op)

        def MSET(out, v):
            return lambda e: e.memset(out, v)

        Bc = Buf("consts")

        def cload(shape, dt, src, eng="sp", **kw):
            t = sb(shape, dt)
            S.dma(eng, DMA(t[:], src, **kw), writes=[Bc])
            return t

        KCUT = int(os.environ.get("KCUT", "99"))
        idb = cload([128, 128], BF16, k_id, "pool")
        dm = cload([128, 4, 128], F32, k_dm)
        qd = cload([128, 4, 128], F32, k_qd)
        kd = cload([128, 4], F32, k_kd)
        cm = cload([128, 128], F32, k_cm)
        lnwT = cload([128, 8], F32, ln_w.rearrange("(c p) -> p c", p=128), allow_slow_non_contiguous=True)
        qnwT = cload([128, 3], F32, q_norm_w.rearrange("(c p) -> p c", p=128), allow_slow_non_contiguous=True)
        gnwT = cload([128, 16], F32, ret_gn_w.rearrange("(c p) -> p c", p=128), allow_slow_non_contiguous=True)
        kvwB = cload([128, 256], F32, kv_norm_w.partition_broadcast(128))
        finB = cload([128, D], F32, fin_w.partition_broadcast(128))
        ones = sb([128, 128], BF16)
        S.op("pool", MSET(ones[:], 1.0), writes=[Bc])
        scr = sb([128, 16], F32); Bscr = Buf("scr")
        S.op("pool", MSET(scr[:], 0.0), writes=[Bscr])
        wuq = sb([128, 3, 1536], BF16)
        S.dma("pool", DMA(wuq[:], w_uq.rearrange("(c p) n -> p c n", p=128)), writes=[Bc])
        wuqs = sb([128, 3, 8, 64], BF16)
        uq4 = w_uq.rearrange("(c p) (h f) -> p c h f", p=128, f=192)
        for c in range(3):
            S.dma("pool", DMA(wuqs[:, c, :, 0:32], uq4[:, c, :, 160:192]), writes=[Bc])
            S.dma("pool", DMA(wuqs[:, c, :, 32:64], uq4[:, c, :, 128:160]), writes=[Bc])
        wukv = sb([128, 2, 2048], BF16)
        S.dma("pool", DMA(wukv[:], w_ukv.rearrange("(c p) n -> p c n", p=128)), writes=[Bc])

        xnT = sb([128, 8, TBM], BF16); BxnT = Buf("xnT")
        moT = sb([128, 8, TBM], BF16); BmoT = [Buf(f"moT{h}") for h in range(8)]
        gaT = sb([128, 16, TBM], BF16); BgaT = [Buf(f"gaT{h}") for h in range(4)]
        mrT = sb([128, 8, TBM], BF16); BmrT = Buf("mrT")
        cqn = sb([128, 3, TBM], BF16); Bcqn = Buf("cqn")
        tabR = sb([128, 2, TBM], BF16); tab2 = sb([64, 2, TBM], BF16); Btab = Buf("tab")
        SM1 = RP([128, 8], F32, 12, "sm1")
        F512 = RP([128, 2, 512], F32, 3, "f512")
        JK = RP([128, D], BF16, 1, "jk")
        XT = RP([128, D], F32, 2, "xt")
        XB = RP([128, D], BF16, 2, "xb")

        def do_barrier():
            if os.environ.get("KNOBAR"):
                return
            p, Bp = PS()
            t1 = {"pe": MM(p[:1, 0:1], [(ones[:1, :1], ones[:1, 0:1])]), "act": ACT(scr[:, 0:1], scr[:, 8:9]),
                  "dve": CP(scr[:, 1:2], scr[:, 9:10]), "pool": MSET(scr[:, 2:3], 0.0), "sp": DMA(scr[0:1, 3:4], scr[0:1, 10:11])}
            t2 = {"pe": MM(p[:1, 1:2], [(ones[:1, :1], ones[:1, 0:1])]), "act": ACT(scr[:, 4:5], scr[:, 11:12]),
                  "dve": CP(scr[:, 5:6], scr[:, 12:13]), "pool": MSET(scr[:, 6:7], 0.0), "sp": DMA(scr[0:1, 7:8], scr[0:1, 13:14])}
            S.barrier(t1, t2)

        @contextlib.contextmanager
        def phase():
            ph = contextlib.ExitStack()
            stk.append(ph)
            try:
                yield
                flush_st()
                do_barrier()
            finally:
                stk.pop()
                ph.close()

        KSTOP = os.environ.get("KSTOP", "")

        def block(kind, tok0, ntok, X):
            smp = kind == "s"
            stop = KSTOP.split(":")[1] if KSTOP.startswith(kind + ":") else "Z"
            if KSTOP and not KSTOP.startswith(kind + ":"):
                if not (KSTOP.startswith("p:") and smp and len(KSTOP.split(":")) > 2):
                    return
            tc0 = SEQ if smp else tok0
            xsrc = xs if smp else xp
            tiles = splits(ntok, 128)
            groups = splits(ntok, 512)

            def wload(col0, ncols, WB):
                w, Bw = WB.get()
                wdma(w[:, :, :ncols], "w_in", col0, ncols, writes=[Bw])
                return w, Bw

            if stop == "0":
                return
            S.dma("pool", DMA(tabR[:, 0, :ntok], k_cR[:, tc0:tc0 + ntok]), writes=[Btab])
            S.dma("pool", DMA(tabR[:, 1, :ntok], k_sR[:, tc0:tc0 + ntok]), writes=[Btab])
            S.dma("pool", DMA(tab2[:, 0, :ntok], k_c2[:, tc0:tc0 + ntok]), writes=[Btab])
            S.dma("pool", DMA(tab2[:, 1, :ntok], k_s2[:, tc0:tc0 + ntok]), writes=[Btab])

            for (o, n) in tiles:
                xt, Bxt = XT.get()
                S.dma("sp", DMA(xt[:n], xsrc[tok0 + o:tok0 + o + n, :]), writes=[Bxt])
                jk, Bjk = JK.get()
                s1, Bs1 = SM1.get()
                S.op("act", ACT(jk[:n], xt[:n], AF.Square, accum_out=s1[:n, 0:1]), reads=[Bxt], writes=[Bjk, Bs1])
                S.op("act", ACT(s1[:n, 1:2], s1[:n, 0:1], AF.Sqrt, scale=1.0 / D, bias=EPS), reads=[Bs1], writes=[Bs1])
                S.op("dve", RECIP(s1[:n, 2:3], s1[:n, 1:2]), reads=[Bs1], writes=[Bs1])
                xb, Bxb = XB.get()
                S.op("dve", TS(xb[:n], xt[:n], s1[:n, 2:3], None, ALU.mult), reads=[Bxt, Bs1], writes=[Bxb])
                p, Bp = PS()
                pv = bfv(p)
                S.op("pe", TR([(pv[:, c * 128:c * 128 + n], xb[:n, c * 128:(c + 1) * 128]) for c in range(8)], idb),
                     reads=[Bxb, Bc], writes=[Bp])
                S.op("dve", TT_(xnT[:, :, o:o + n], pv.rearrange("p (c t) -> p c t", c=8)[:, :, :n],
                                lnwT[:].unsqueeze(2).to_broadcast([128, 8, n]), ALU.mult),
                     reads=[Bp, Bc], writes=[BxnT])

            if stop == "A":
                return
            with phase():
                cqT = sb([128, 3, TBM], F32); BcqT = Buf("cqT")
                sqT = sb([128, 3, TBM], BF16); BsqT = Buf("sqT")
                CKO = RP([128, 320], F32, 2, "cko")
                CKB = RP([128, 320], BF16, 2, "ckb")
                CSP = RP([128, 64], F32, 2, "csp")
                WB = RP([128, 8, 512], BF16, 2, "wb")
                w, Bw = wload(C_CQ, 384, WB)
                for j in range(3):
                    for (o, n) in groups:
                        p, Bp = PS()
                        S.op("pe", MM(p[:, :n], [(w[:, k, j * 128:(j + 1) * 128], xnT[:, k, o:o + n]) for k in range(8)]),
                             reads=[Bw, BxnT], writes=[Bp])
                        S.op("act", ACT(cqT[:, j, o:o + n], p[:, :n]), reads=[Bp], writes=[BcqT])
                        S.op("act", ACT(sqT[:, j, o:o + n], p[:, :n], AF.Square), reads=[Bp], writes=[BsqT])
                for (o, n) in groups:
                    p, Bp = PS()
                    S.op("pe", MM(p[:, :n], [(ones[:], sqT[:, j, o:o + n]) for j in range(3)]), reads=[BsqT, Bc], writes=[Bp])
                    f5, Bf5 = F512.get()
                    S.op("act", ACT(f5[:, 0, :n], p[:, :n], AF.Sqrt, scale=1.0 / 384, bias=EPS), reads=[Bp], writes=[Bf5])
                    S.op("dve", RECIP(f5[:, 1, :n], f5[:, 0, :n]), reads=[Bf5], writes=[Bf5])
                    for j in range(3):
                        S.op("dve", STT(cqn[:, j, o:o + n], cqT[:, j, o:o + n], qnwT[:, j:j + 1], f5[:, 1, :n], ALU.mult, ALU.mult),
                             reads=[BcqT, Bf5, Bc], writes=[Bcqn])
                w, Bw = wload(C_CKV, 320, WB)
                for ti, (o, n) in enumerate(tiles):
                    p, Bp = PS()
                    S.op("pe", MM(p[:n, :320], [(xnT[:, k, o:o + n], w[:, k, :320]) for k in range(8)]),
                         reads=[Bw, BxnT], writes=[Bp])
                    ot, Bot = CKO.get()
                    jk, Bjk = JK.get()
                    s1, Bs1 = SM1.get()
                    S.op("act", ACT(jk[:n, :256], p[:n, :256], AF.Square, accum_out=s1[:n, 0:1]), reads=[Bp], writes=[Bjk, Bs1])
                    S.op("act", ACT(s1[:n, 1:2], s1[:n, 0:1], AF.Sqrt, scale=1.0 / 256, bias=EPS), reads=[Bs1], writes=[Bs1])
                    S.op("dve", RECIP(s1[:n, 2:3], s1[:n, 1:2]), reads=[Bs1], writes=[Bs1])
                    S.op("dve", STT(ot[:n, 0:256], p[:n, 0:256], s1[:n, 2:3], kvwB[:n, :], ALU.mult, ALU.mult),
                         reads=[Bp, Bs1, Bc], writes=[Bot])
                    cst, Bcst = CSP.get()
                    S.dma("sp", DMA(cst[:n, 0:32], k_cM[tc0 + o:tc0 + o + n, :]), writes=[Bcst])
                    S.dma("sp", DMA(cst[:n, 32:64], k_sM[tc0 + o:tc0 + o + n, :]), reads=[Bcst], writes=[Bcst])
                    flush_st()
                    cs_ = cst[:n, 0:32]
                    sn_ = cst[:n, 32:64]
                    f5, Bf5 = F512.get()
                    S.op("dve", TT_(f5[:n, 0, 0:32], p[:n, 256:288], cs_, ALU.mult), reads=[Bp, Bcst], writes=[Bf5])
                    S.op("dve", TT_(f5[:n, 0, 32:64], p[:n, 288:320], sn_, ALU.mult), reads=[Bp, Bcst, Bf5], writes=[Bf5])
                    S.op("dve", TT_(f5[:n, 0, 64:96], p[:n, 288:320], cs_, ALU.mult), reads=[Bp, Bcst, Bf5], writes=[Bf5])
                    S.op("dve", TT_(f5[:n, 0, 96:128], p[:n, 256:288], sn_, ALU.mult), reads=[Bp, Bcst, Bf5], writes=[Bf5])
                    S.op("dve", TT_(ot[:n, 256:288], f5[:n, 0, 0:32], f5[:n, 0, 32:64], ALU.subtract), reads=[Bf5, Bot], writes=[Bot])
                    S.op("dve", TT_(ot[:n, 288:320], f5[:n, 0, 64:96], f5[:n, 0, 96:128], ALU.add), reads=[Bf5, Bot], writes=[Bot])
                    if smp:
                        store(ocks[o:o + n, :], ot[:n, 0:256], [Bot])
                        store(okrs[o:o + n, :], ot[:n, 256:320], [Bot])
                    else:
                        store(ockp[tok0 + o:tok0 + o + n, :], ot[:n, 0:256], [Bot])
                        store(okrp[tok0 + o:tok0 + o + n, :], ot[:n, 256:320], [Bot])
                    ob, Bob = CKB.get()
                    S.op("act", ACT(ob[:n, :], ot[:n, :]), reads=[Bot], writes=[Bob])
                    if smp:
                        S.op("dve", CP(X["ckS"][:n, :], ob[:n, 0:256]), reads=[Bob], writes=[X["Bcks"]])
                    p2, Bp2 = PS()
                    pv2 = bfv(p2)
                    S.op("pe", TR([(pv2[:, 0:n], ob[:n, 0:128]), (pv2[:, 128:128 + n], ob[:n, 128:256]),
                                   (pv2[:64, 256:256 + n], ob[:n, 256:320])], idb), reads=[Bob, Bc], writes=[Bp2])
                    if smp:
                        S.op("act", ACT(X["ckTs"][:, :, :n], pv2[:, 0:256].rearrange("p (c t) -> p c t", c=2)[:, :, :n]), reads=[Bp2], writes=[X["Bcks"]])
                        S.op("act", ACT(X["krTs"][:, :n], pv2[:64, 256:256 + n]), reads=[Bp2, X["Bcks"]], writes=[X["Bcks"]])
                    else:
                        gt = (tok0 + o) // 128
                        S.op("act", ACT(X["ckT"][:, :, tok0 + o:tok0 + o + n], pv2[:, 0:256].rearrange("p (c t) -> p c t", c=2)[:, :, :n]),
                             reads=[Bp2], writes=[X["BckT"][gt]])
                        S.op("act", ACT(X["krT"][:, tok0 + o:tok0 + o + n], pv2[:64, 256:256 + n]), reads=[Bp2], writes=[X["BkrT"][gt]])

            if stop == "B":
                return
            with phase():
                QN = RP([128, TBM], BF16, 2, "qn")
                QR = RP([64, TBM], BF16, 2, "qr")
                WM = RP([128, 8, 128], BF16, 2, "wm")
                MG = RP([128, TBM], BF16, 2, "mg")
                ON = RP([128, 128], BF16, 2, "on")
                if smp:
                    sample_attn = make_sample_attn(X)
                else:
                    KN = RP([128, SEQ], BF16, 1, "kn")
                    VH = RP([128, SEQ // 128, 128], BF16, 1, "vh")
                    PM = RP([128, SEQ], BF16, 2, "pm")
                    PTT = RP([128, SEQ // 128, 128], BF16, 1, "ptt")
                mgs = []
                for h in range(8):
                    wm, Bwm = WM.get()
                    wdma(wm[:], "w_in", C_MG + h * 128, 128, writes=[Bwm])
                    mg, Bmg = MG.get()
                    for (o, n) in groups:
                        p, Bp = PS()
                        S.op("pe", MM(p[:, :n], [(wm[:, k, :], xnT[:, k, o:o + n]) for k in range(8)]), reads=[Bwm, BxnT], writes=[Bp])
                        S.op("act", ACT(mg[:, o:o + n], p[:, :n], AF.Silu), reads=[Bp], writes=[Bmg])
                    qn, Bqn = QN.get()
                    qr, Bqr = QR.get()
                    for (o, n) in groups:
                        p, Bp = PS()
                        S.op("pe", MM(p[:, :n], [(wuq[:, k, h * 192:h * 192 + 128], cqn[:, k, o:o + n]) for k in range(3)]),
                             reads=[Bc, Bcqn], writes=[Bp])
                        S.op("act", ACT(qn[:, o:o + n], p[:, :n]), reads=[Bp], writes=[Bqn])
                        p, Bp = PS()
                        S.op("pe", MMS([(p[:64, :n], [(wuq[:, k, h * 192 + 128:h * 192 + 192], cqn[:, k, o:o + n]) for k in range(3)]),
                                        (p[64:128, :n], [(wuqs[:, k, h, :], cqn[:, k, o:o + n]) for k in range(3)])]),
                             reads=[Bc, Bcqn], writes=[Bp])
                        f5, Bf5 = F512.get()
                        S.op("dve", TT_(f5[:64, 0, :n], p[:64, :n], tab2[:, 0, o:o + n], ALU.mult), reads=[Bp, Btab], writes=[Bf5])
                        S.op("dve", TT_(f5[:64, 1, :n], p[64:128, :n], tab2[:, 1, o:o + n], ALU.mult), reads=[Bp, Btab, Bf5], writes=[Bf5])
                        S.op("dve", TT_(qr[:, o:o + n], f5[:64, 0, :n], f5[:64, 1, :n], ALU.add), reads=[Bf5], writes=[Bqr])
                    if smp:
                        sample_attn(h, qn, Bqn, qr, Bqr, mg, Bmg)
                        continue
                    ckT, krT, BckT, BkrT = X["ckT"], X["krT"], X["BckT"], X["BkrT"]
                    nkey = tok0 + ntok
                    kn, Bkn = KN.get()
                    vh, Bvh = VH.get()
                    for (o, n) in splits(nkey, 512):
                        p, Bp = PS()
                        S.op("pe", MM(p[:, :n], [(wukv[:, c, h * 256:h * 256 + 128], ckT[:, c, o:o + n]) for c in range(2)]),
                             reads=[Bc] + BckT[o // 128:(o + n) // 128], writes=[Bp])
                        S.op("act", ACT(kn[:, o:o + n], p[:, :n]), reads=[Bp], writes=[Bkn])
                    for g4 in splits(nkey // 128, 4):
                        p, Bp = PS()
                        S.op("pe", MMS([(p[:, i * 128:(i + 1) * 128],
                                         [(ckT[:, c, (g4[0] + i) * 128:(g4[0] + i + 1) * 128], wukv[:, c, h * 256 + 128:h * 256 + 256]) for c in range(2)])
                                        for i in range(g4[1])]),
                             reads=[Bc] + BckT[g4[0]:g4[0] + g4[1]], writes=[Bp])
                        S.op("act", ACT(vh[:, g4[0]:g4[0] + g4[1], :], p[:, :g4[1] * 128].rearrange("p (t e) -> p t e", e=128)),
                             reads=[Bp], writes=[Bvh])
                    def partA(o, n, qn=qn, Bqn=Bqn, qr=qr, Bqr=Bqr, kn=kn, Bkn=Bkn):
                        qt = (tok0 + o) // 128
                        nk = (qt + 1) * 128
                        kgs = splits(nk, 512)
                        ng = len(kgs)
                        s1, Bs1 = SM1.get()
                        pss = []
                        for gi, (ko, kn_) in enumerate(kgs):
                            p, Bp = PS()
                            pss.append((p, Bp, ko, kn_))
                            S.op("pe", MM(p[:, :kn_], [(qn[:, o:o + n], kn[:, ko:ko + kn_]), (qr[:, o:o + n], krT[:, ko:ko + kn_])]),
                                 reads=[Bqn, Bqr, Bkn] + BkrT[ko // 128:(ko + kn_) // 128], writes=[Bp])
                            if gi == ng - 1:
                                S.op("dve", TT_(p[:, kn_ - 128:kn_], p[:, kn_ - 128:kn_], cm[:], ALU.add), reads=[Bp, Bc], writes=[Bp])
                            S.op("dve", RED(s1[:, gi:gi + 1], p[:, :kn_], ALU.max), reads=[Bp, Bs1], writes=[Bs1])
                        s2_, Bs2 = SM1.get()
                        S.op("dve", RED(s2_[:, 0:1], s1[:, 0:ng], ALU.max), reads=[Bs1], writes=[Bs2])
                        S.op("dve", TS(s2_[:, 1:2], s2_[:, 0:1], -SCALE, None, ALU.mult), reads=[Bs2], writes=[Bs2])
                        pm, Bpm = PM.get()
                        s3, Bs3 = SM1.get()
                        for gi, (p, Bp, ko, kn_) in enumerate(pss):
                            S.op("act", ACT(pm[:, ko:ko + kn_], p[:, :kn_], AF.Exp, scale=SCALE, bias=s2_[:, 1:2], accum_out=s3[:, gi:gi + 1]),
                                 reads=[Bp, Bs2, Bs3], writes=[Bpm, Bs3])
                        S.op("dve", RED(s3[:, 4:5], s3[:, 0:ng], ALU.add), reads=[Bs3], writes=[Bs3])
                        S.op("dve", RECIP(s3[:, 5:6], s3[:, 4:5]), reads=[Bs3], writes=[Bs3])
                        return dict(o=o, n=n, nk=nk, pm=pm, Bpm=Bpm, s3=s3, Bs3=Bs3)

                    def partB(a, vh=vh, Bvh=Bvh, mg=mg, Bmg=Bmg, h=h):
                        o, n, nk, pm, Bpm, s3, Bs3 = a["o"], a["n"], a["nk"], a["pm"], a["Bpm"], a["s3"], a["Bs3"]
                        pt_, Bpt = PTT.get()
                        nkt = nk // 128
                        for g8 in splits(nkt, 8):
                            p, Bp = PS()
                            pv = bfv(p)
                            S.op("pe", TR([(pv[:, i * 128:(i + 1) * 128], pm[:, (g8[0] + i) * 128:(g8[0] + i + 1) * 128]) for i in range(g8[1])], idb),
                                 reads=[Bpm, Bc], writes=[Bp])
                            S.op("dve", CP(pt_[:, g8[0]:g8[0] + g8[1], :], pv[:, :g8[1] * 128].rearrange("p (t q) -> p t q", q=128)),
                                 reads=[Bp, Bpt], writes=[Bpt])
                        po, Bpo = PS()
                        S.op("pe", MM(po[:n, :128], [(pt_[:, kt, :n], vh[:, kt, :]) for kt in range(nkt)]), reads=[Bpt, Bvh], writes=[Bpo])
                        on_, Bon = ON.get()
                        S.op("dve", TS(on_[:n, :128], po[:n, :128], s3[:n, 5:6], None, ALU.mult), reads=[Bpo, Bs3], writes=[Bon])
                        p, Bp = PS()
                        pv = bfv(p)
                        S.op("pe", TR([(pv[:, :n], on_[:n, :128])], idb), reads=[Bon, Bc], writes=[Bp])
                        S.op("dve", TT_(moT[:, h, o:o + n], pv[:, :n], mg[:, o:o + n], ALU.mult), reads=[Bp, Bmg], writes=[BmoT[h]])

                    prevA = None
                    for (o, n) in tiles:
                        curA = partA(o, n)
                        if prevA is not None:
                            partB(prevA)
                        prevA = curA
                    partB(prevA)

            if stop == "C":
                return
            with phase():
                WB = RP([128, 8, 512], BF16, 4, "wb")
                QT = RP([128, 2, TBM], BF16, 1, "qT")
                KT = RP([128, 2, TBM], BF16, 1, "kT")
                VT = RP([128, NTB, 512], BF16, 1, "vT")
                GT = RP([128, NTB, 512], BF16, 1, "gT")
                STP = RP([128, 128], BF16, 2, "sT")
                QPP = RP([128, 2, 128], BF16, 2, "qp")
                KD = RP([128, 256], BF16, 2, "kd")
                BNS = RP([128, 6], F32, 2, "bns")
                GG = RP([128, 512], BF16, 2, "gg")
                ON5 = RP([128, 512], BF16, 2, "on5")
                if smp:
                    QF = RP([128, 2, NS], F32, 1, "qf")
                    QMB = RP([128, 2, NS], BF16, 2, "qmb")
                    SIB = RP([128, 2, 512], BF16, 2, "sib")
                    KDMB = RP([NS, 256], BF16, 2, "kdmb")
                    SIN = RP([128, 2, 512], F32, 3, "sin")
                    SOUT = RP([128, 2, 512], F32, 3, "sout")
                    oacc = sb([128, 512], F32); Boacc = Buf("oacc")
                for h in range(4):
                    g = float(np.exp(LG[h]))
                    qT, BqT = QT.get()
                    kT, BkT = KT.get()
                    for (dst, Bdst, col, sc) in ((qT, BqT, C_RQ + h * 256, 1.0), (kT, BkT, C_RK + h * 256, 1.0 / 16)):
                        w, Bw = wload(col, 256, WB)
                        for (o, n) in groups:
                            pa, Bpa = PS()
                            pb_, Bpb = PS()
                            S.op("pe", MM(pa[:, :n], [(w[:, k, 0:128], xnT[:, k, o:o + n]) for k in range(8)]), reads=[Bw, BxnT], writes=[Bpa])
                            S.op("pe", MM(pb_[:, :n], [(w[:, k, 128:256], xnT[:, k, o:o + n]) for k in range(8)]), reads=[Bw, BxnT], writes=[Bpb])
                            xe, Bxe = F512.get()
                            S.op("act", ACT(xe[:, 0, :n], pa[:, :n], AF.Copy, scale=sc), reads=[Bpa], writes=[Bxe])
                            S.op("act", ACT(xe[:, 1, :n], pb_[:, :n], AF.Copy, scale=sc), reads=[Bpb, Bxe], writes=[Bxe])
                            fa, Bfa = F512.get()
                            fb, Bfb = F512.get()
                            csl = tabR[:, 0, o:o + n]
                            snl = tabR[:, 1, o:o + n]
                            S.op("dve", TT_(fa[:, :, :n], xe[:, :, :n], csl.unsqueeze(1).to_broadcast([128, 2, n]), ALU.mult), reads=[Bxe, Btab], writes=[Bfa])
                            S.op("dve", TT_(fb[:, 0, :n], xe[:, 1, :n], snl, ALU.mult), reads=[Bxe, Btab], writes=[Bfb])
                            S.op("dve", TT_(fb[:, 1, :n], xe[:, 0, :n], snl, ALU.mult), reads=[Bxe, Btab, Bfb], writes=[Bfb])
                            S.op("dve", TT_(dst[:, 0, o:o + n], fa[:, 0, :n], fb[:, 0, :n], ALU.subtract), reads=[Bfa, Bfb, Bdst], writes=[Bdst])
                            S.op("dve", TT_(dst[:, 1, o:o + n], fa[:, 1, :n], fb[:, 1, :n], ALU.add), reads=[Bfa, Bfb, Bdst], writes=[Bdst])
                    vT, BvT = VT.get()
                    gT, BgT = GT.get()
                    for (dst, Bdst, col, fn_) in ((vT, BvT, C_RV + h * 512, AF.Copy), (gT, BgT, C_RG + h * 512, AF.Silu)):
                        w, Bw = wload(col, 512, WB)
                        for ti, (o, n) in enumerate(tiles):
                            p, Bp = PS()
                            S.op("pe", MM(p[:n, :], [(xnT[:, k, o:o + n], w[:, k, :]) for k in range(8)]), reads=[Bw, BxnT], writes=[Bp])
                            S.op("act", ACT(dst[:n, ti, :], p[:n, :], fn_), reads=[Bp, Bdst], writes=[Bdst])
                    d_tails = []
                    for ti, (o, n) in enumerate(tiles):
                        first = (not smp) and tok0 == 0 and ti == 0
                        p, Bp = PS()
                        S.op("pe", MM(p[:n, :n], [(kT[:, c, o:o + n], qT[:, c, o:o + n]) for c in range(2)]), reads=[BkT, BqT], writes=[Bp])
                        sT, BsT = STP.get()
                        mask = X["dms"][:n, h, :n] if smp else dm[:n, h, :n]
                        S.op("dve", TT_(sT[:n, :n], p[:n, :n], mask, ALU.mult), reads=[Bp, Bc], writes=[BsT])
                        po, Bpo = PS()
                        kdt, Bkdt = KD.get()
                        p2, Bp2 = PS()
                        pv2 = bfv(p2)
                        S.op("pe", TR([(pv2[:n, c * 128:(c + 1) * 128], kT[:, c, o:o + n]) for c in range(2)], idb), reads=[BkT, Bc], writes=[Bp2])
                        kdsc = X["kds"][:n, h:h + 1] if smp else kd[:n, h:h + 1]
                        S.op("act", ACT(kdt[:n, :], pv2[:n, 0:256], AF.Copy, scale=kdsc), reads=[Bp2, Bc], writes=[Bkdt])
                        if smp:
                            qf, Bqf = QF.get()
                            S.op("dve", TT_(qf[:, :, :n], qT[:, :, o:o + n], X["qds"][:, h, :n].unsqueeze(1).to_broadcast([128, 2, n]), ALU.mult),
                                 reads=[BqT, Bc], writes=[Bqf])
                            S.op("pe", MM(po[:n, :], [(sT[:n, :n], vT[:n, ti, :])]), reads=[BsT, BvT], writes=[Bpo])
                            S.op("act", ACT(oacc[:n, :], po[:n, :]), reads=[Bpo], writes=[Boacc])
                            pc, Bpc = PS()
                            ps_avoid.append(Bpc)
                            for b in range(NSQ):
                                si, Bsi = SIN.get()
                                r0 = (b * 4 + h) * 256
                                S.dma("sp", DMA(si[:], stt[r0:r0 + 256, :].rearrange("(c p) n -> p c n", p=128)), writes=[Bsi])
                                flush_st()
                                sib, Bsib = SIB.get()
                                S.op("act", ACT(sib[:], si[:]), reads=[Bsi], writes=[Bsib])
                                qm, Bqm = QMB.get()
                                S.op("dve", TT_(qm[:, :, :n], qf[:, :, :n], X["bm"][:, b, :n].unsqueeze(1).to_broadcast([128, 2, n]), ALU.mult),
                                     reads=[Bqf, Bc], writes=[Bqm])

                                def cross(e, qm=qm, sib=sib, b=b, pc=pc, n=n):
                                    r = None
                                    for c in range(2):
                                        r = e.matmul(pc[:n, :], lhsT=qm[:, c, :n], rhs=sib[:, c, :],
                                                     start=(b == 0 and c == 0), stop=(b == NSQ - 1 and c == 1))
                                    return r
                                S.op("pe", cross, reads=[Bqm, Bsib], writes=[Bpc])
                                if b == NSQ - 1:
                                    S.op("dve", TT_(oacc[:n, :], oacc[:n, :], pc[:n, :], ALU.add), reads=[Bpc, Boacc], writes=[Boacc])
                                    ps_avoid.remove(Bpc)
                                kdm, Bkdm = KDMB.get()
                                S.op("dve", TS(kdm[:n, :], kdt[:n, :], X["rm"][:n, b:b + 1], None, ALU.mult), reads=[Bkdt, Bc], writes=[Bkdm])
                                pu, Bpu = PS()
                                pu2, Bpu2 = PS()
                                S.op("pe", MM(pu[:, :], [(kdm[:n, 0:128], vT[:n, ti, :])]), reads=[Bkdm, BvT], writes=[Bpu])
                                S.op("pe", MM(pu2[:, :], [(kdm[:n, 128:256], vT[:n, ti, :])]), reads=[Bkdm, BvT], writes=[Bpu2])
                                so, Bso = SOUT.get()
                                S.op("dve", STT(so[:, 0, :], si[:, 0, :], g ** 4, pu[:, :], ALU.mult, ALU.add), reads=[Bsi, Bpu], writes=[Bso])
                                S.op("dve", STT(so[:, 1, :], si[:, 1, :], g ** 4, pu2[:, :], ALU.mult, ALU.add), reads=[Bsi, Bpu2, Bso], writes=[Bso])
                                store(orets[r0:r0 + 256, :].rearrange("(c p) n -> p c n", p=128), so[:], [Bso])
                            osrc, Bosrc = oacc, Boacc
                        else:
                            Sst, Sbf, BSst, BSbf = X["Sst"], X["Sbf"], X["BSst"], X["BSbf"]
                            pairs = [(sT[:n, :n], vT[:n, ti, :])]
                            rd = [BsT, BvT]
                            if not first:
                                qp, Bqp = QPP.get()
                                S.op("dve", TT_(qp[:, :, :n], qT[:, :, o:o + n], qd[:, h, :n].unsqueeze(1).to_broadcast([128, 2, n]), ALU.mult),
                                     reads=[BqT, Bc], writes=[Bqp])
                                pairs += [(qp[:, c, :n], Sbf[:, h, c, :]) for c in range(2)]
                                rd += [Bqp, BSbf[h]]
                            S.op("pe", MM(po[:n, :], pairs), reads=rd, writes=[Bpo])
                            osrc, Bosrc = po, Bpo
                            pu, Bpu = PS()
                            pu2, Bpu2 = PS()
                            S.op("pe", MM(pu[:, :], [(kdt[:n, 0:128], vT[:n, ti, :])]), reads=[Bkdt, BvT], writes=[Bpu])
                            S.op("pe", MM(pu2[:, :], [(kdt[:n, 128:256], vT[:n, ti, :])]), reads=[Bkdt, BvT], writes=[Bpu2])
                            if first:
                                S.op("dve", CP(Sst[:, h, 0, :], pu[:, :]), reads=[Bpu, BSst[h]], writes=[BSst[h]])
                                S.op("dve", CP(Sst[:, h, 1, :], pu2[:, :]), reads=[Bpu2, BSst[h]], writes=[BSst[h]])
                            else:
                                S.op("dve", STT(Sst[:, h, 0, :], Sst[:, h, 0, :], g ** 128, pu[:, :], ALU.mult, ALU.add), reads=[Bpu, BSst[h]], writes=[BSst[h]])
                                S.op("dve", STT(Sst[:, h, 1, :], Sst[:, h, 1, :], g ** 128, pu2[:, :], ALU.mult, ALU.add), reads=[Bpu2, BSst[h]], writes=[BSst[h]])
                            S.op("act", ACT(Sbf[:, h, :, :], Sst[:, h, :, :]), reads=[BSst[h]], writes=[BSbf[h]])
                            if tok0 + o + n == SEQ:
                                S.dma("sp", DMA(oretp[h * 256:(h + 1) * 256, :].rearrange("(c p) n -> p c n", p=128), Sst[:, h, :, :]), reads=[BSst[h]])
                        s1, Bs1 = SM1.get()
                        bs, Bbs = BNS.get()
                        S.op("dve", lambda e, bs=bs, osrc=osrc, n=n: e.bn_stats(out=bs[:n, :], in_=osrc[:n, :]), reads=[Bosrc], writes=[Bbs])
                        S.op("dve", lambda e, bs=bs, s1=s1, n=n: e.bn_aggr(out=s1[:n, 0:2], in_=bs[:n, :]), reads=[Bbs], writes=[Bs1])
                        S.op("act", ACT(s1[:n, 2:3], s1[:n, 1:2], AF.Sqrt, scale=1.0, bias=EPS), reads=[Bs1], writes=[Bs1])
                        S.op("dve", RECIP(s1[:n, 3:4], s1[:n, 2:3]), reads=[Bs1], writes=[Bs1])
                        on_, Bon = ON5.get()
                        S.op("dve", TS(on_[:n, :], osrc[:n, :], s1[:n, 0:1], s1[:n, 3:4], ALU.subtract, ALU.mult), reads=[Bosrc, Bs1], writes=[Bon])
                        gg, Bgg = GG.get()
                        S.op("dve", TT_(gg[:n, :], on_[:n, :], gT[:n, ti, :], ALU.mult), reads=[Bon, BgT], writes=[Bgg])
                        def tail(gg=gg, Bgg=Bgg, o=o, n=n, h=h):
                            p3, Bp3 = PS()
                            pv3 = bfv(p3)
                            S.op("pe", TR([(pv3[:, c * 128:c * 128 + n], gg[:n, c * 128:(c + 1) * 128]) for c in range(4)], idb), reads=[Bgg, Bc], writes=[Bp3])
                            for c in range(4):
                                S.op("act", ACT(gaT[:, h * 4 + c, o:o + n], pv3[:, c * 128:c * 128 + n], AF.Copy, scale=gnwT[:, h * 4 + c:h * 4 + c + 1]),
                                     reads=[Bp3, Bc, BgaT[h]], writes=[BgaT[h]])
                        d_tails.append(tail)
                        if len(d_tails) > 1:
                            d_tails.pop(0)()
                    while d_tails:
                        d_tails.pop(0)()

            if stop == "D":
                return
            with phase():
                WD = RP([128, 40, 128], BF16, 2, "wd")
                for cb in range(8):
                    wd, Bwd = WD.get()
                    cs = slice(cb * 128, (cb + 1) * 128)
                    wdma(wd[:, 0:16, :], "w_down_ret", cb * 128, 128, writes=[Bwd])
                    wdma(wd[:, 16:24, :], "w_down_mla", cb * 128, 128, reads=[Bwd], writes=[Bwd])
                    wdma(wd[:, 24:32, :], "w_in", C_GA + cb * 128, 128, reads=[Bwd], writes=[Bwd])
                    wdma(wd[:, 32:40, :], "w_in", C_GB + cb * 128, 128, reads=[Bwd], writes=[Bwd])
                    for (o, n) in groups:
                        p1, B1 = PS(); p2, B2 = PS(); p3, B3 = PS(); p4, B4 = PS()
                        S.op("pe", MM(p1[:, :n], [(wd[:, k, :], gaT[:, k, o:o + n]) for k in range(16)]), reads=[Bwd] + BgaT, writes=[B1])
                        S.op("pe", MM(p2[:, :n], [(wd[:, 16 + k, :], moT[:, k, o:o + n]) for k in range(8)]), reads=[Bwd] + BmoT, writes=[B2])
                        S.op("pe", MM(p3[:, :n], [(wd[:, 24 + k, :], xnT[:, k, o:o + n]) for k in range(8)]), reads=[Bwd, BxnT], writes=[B3])
                        S.op("pe", MM(p4[:, :n], [(wd[:, 32 + k, :], xnT[:, k, o:o + n]) for k in range(8)]), reads=[Bwd, BxnT], writes=[B4])
                        f5, Bf5 = F512.get()
                        S.op("act", ACT(f5[:, 0, :n], p3[:, :n], AF.Sigmoid), reads=[B3], writes=[Bf5])
                        S.op("act", ACT(f5[:, 1, :n], p4[:, :n], AF.Sigmoid), reads=[B4, Bf5], writes=[Bf5])
                        f6, Bf6 = F512.get()
                        S.op("dve", TT_(f6[:, 0, :n], p1[:, :n], f5[:, 0, :n], ALU.mult), reads=[B1, Bf5], writes=[Bf6])
                        S.op("dve", TT_(f6[:, 1, :n], p2[:, :n], f5[:, 1, :n], ALU.mult), reads=[B2, Bf5, Bf6], writes=[Bf6])
                        S.op("dve", TT_(mrT[:, cb, o:o + n], f6[:, 0, :n], f6[:, 1, :n], ALU.add), reads=[Bf6, BmrT], writes=[BmrT])

            if stop == "E":
                return
            with phase():
                WF = RP([128, 18, D], BF16, 1, "wf")
                PLT = RP([128, 256], BF16, 2, "plt")
                XN = RP([128, D], F32, 2, "xn")
                XTT = RP([128, 10, 128], BF16, 2, "xtt")
                wo, Bwo = WF.get()
                wdma(wo[:, 0:8, :], "w_out", 0, D, writes=[Bwo])
                wdma(wo[:, 8:16, :], "w_ple_gate", 0, D, reads=[Bwo], writes=[Bwo])
                wdma(wo[:, 16:18, :], "w_ple_proj", 0, D, reads=[Bwo], writes=[Bwo])
                psrc = psl if smp else ppl
                for (o, n) in tiles:
                    xt, Bxt = XT.get()
                    S.dma("sp", DMA(xt[:n], xsrc[tok0 + o:tok0 + o + n, :]), writes=[Bxt])
                    pl, Bpl = PLT.get()
                    S.dma("pool", DMA(pl[:n, :], psrc[tok0 + o:tok0 + o + n, :]), writes=[Bpl])
                    flush_st()
                    pa, Bpa = PS(); pb_, Bpb = PS()
                    S.op("pe", MM(pa[:n, :], [(mrT[:, k, o:o + n], wo[:, k, 0:512]) for k in range(8)]), reads=[BmrT, Bwo], writes=[Bpa])
                    S.op("pe", MM(pb_[:n, :], [(mrT[:, k, o:o + n], wo[:, k, 512:1024]) for k in range(8)]), reads=[BmrT, Bwo], writes=[Bpb])
                    xn, Bxn = XN.get()
                    S.op("dve", TT_(xn[:n, 0:512], pa[:n, :], xt[:n, 0:512], ALU.add), reads=[Bpa, Bxt], writes=[Bxn])
                    S.op("dve", TT_(xn[:n, 512:1024], pb_[:n, :], xt[:n, 512:1024], ALU.add), reads=[Bpb, Bxt, Bxn], writes=[Bxn])
                    xb, Bxb = XB.get()
                    S.op("act", ACT(xb[:n, :], xn[:n, :]), reads=[Bxn], writes=[Bxb])
                    p, Bp = PS()
                    pv = bfv(p)
                    S.op("pe", TR([(pv[:, c * 128:c * 128 + n], xb[:n, c * 128:(c + 1) * 128]) for c in range(8)], idb), reads=[Bxb, Bc], writes=[Bp])
                    xTt, BxTt = XTT.get()
                    S.op("act", ACT(xTt[:, 0:8, :n], pv.rearrange("p (c t) -> p c t", c=8)[:, :, :n]), reads=[Bp], writes=[BxTt])
                    p, Bp = PS()
                    pv = bfv(p)
                    S.op("pe", TR([(pv[:, c * 128:c * 128 + n], pl[:n, c * 128:(c + 1) * 128]) for c in range(2)], idb), reads=[Bpl, Bc], writes=[Bp])
                    S.op("act", ACT(xTt[:, 8:10, :n], pv[:, 0:256].rearrange("p (c t) -> p c t", c=2)[:, :, :n]), reads=[Bp, BxTt], writes=[BxTt])
                    yt, Byt = XN.get()
                    s1, Bs1 = SM1.get()
                    for hf in range(2):
                        pg, Bpg = PS(); pp_, Bpp = PS()
                        S.op("pe", MM(pg[:n, :], [(xTt[:, k, :n], wo[:, 8 + k, hf * 512:(hf + 1) * 512]) for k in range(8)]), reads=[BxTt, Bwo], writes=[Bpg])
                        S.op("pe", MM(pp_[:n, :], [(xTt[:, 8 + k, :n], wo[:, 16 + k, hf * 512:(hf + 1) * 512]) for k in range(2)]), reads=[BxTt, Bwo], writes=[Bpp])
                        f5, Bf5 = F512.get()
                        S.op("act", ACT(f5[:n, 0, :], pg[:n, :], AF.Sigmoid), reads=[Bpg], writes=[Bf5])
                        S.op("dve", TT_(f5[:n, 1, :], f5[:n, 0, :], pp_[:n, :], ALU.mult), reads=[Bf5, Bpp], writes=[Bf5])
                        S.op("dve", TT_(yt[:n, hf * 512:(hf + 1) * 512], f5[:n, 1, :], xn[:n, hf * 512:(hf + 1) * 512], ALU.add),
                             reads=[Bf5, Bxn, Byt], writes=[Byt])
                    jk, Bjk = JK.get()
                    S.op("act", ACT(jk[:n], yt[:n], AF.Square, accum_out=s1[:n, 0:1]), reads=[Byt], writes=[Bjk, Bs1])
                    S.op("act", ACT(s1[:n, 1:2], s1[:n, 0:1], AF.Sqrt, scale=1.0 / D, bias=EPS), reads=[Bs1], writes=[Bs1])
                    S.op("dve", RECIP(s1[:n, 2:3], s1[:n, 1:2]), reads=[Bs1], writes=[Bs1])
                    yo, Byo = XT.get()
                    S.op("dve", STT(yo[:n, :], yt[:n, :], s1[:n, 2:3], finB[:n, :], ALU.mult, ALU.mult), reads=[Byt, Bs1, Bc], writes=[Byo])
                    ydst = ys if smp else yp
                    store(ydst[tok0 + o:tok0 + o + n, :], yo[:n, :], [Byo])

        def make_sample_attn(X):
            RC = 8
            PK = NPG
            assert PK <= 128 and RC == 8
            NCHK = 128 // RC
            NKC = RC * PK
            qlat = sb([128, 2, NSQ, 32], BF16); Bqlat = Buf("qlat")
            qrs = sb([64, NSQ, 32], BF16); Bqrs = Buf("qrs")
            olT = sb([128, 2, 8, NS], BF16); BolT = Buf("olT")
            mgall = sb([128, 8, NS], BF16); Bmgall = Buf("mgall")
            FST = RP([32, 8], F32, 2, "fst")
            FO = RP([32, 256], F32, 2, "fo")
            CG = RP([128, RC, 256], BF16, 5, "cg")
            RG = RP([128, RC, 64], BF16, 5, "rg")
            CTP = RP([128, NKC], BF16, 2, "ct")
            CTP1 = RP([128, NKC], BF16, 2, "ct1")
            RTP = RP([64, NKC], BF16, 2, "rt")
            PMS = RP([32, NKC], BF16, 2, "pms")
            PTS = RP([128, RC * 32], BF16, 2, "pts")
            OLB = RP([32, 256], BF16, 2, "olb")
            PIX = RP([128, 1], I32, 3, "pix")
            wukT, ckTs, krTs, ckS, Bcks, smk = X["wukT"], X["ckTs"], X["krTs"], X["ckS"], X["Bcks"], X["sm"]
            sci = [0]
            tri = [0]
            BPV = Buf("pv7")

            def SCB():
                k = sci[0] % 4
                sci[0] += 1
                return pbank[k], pbuf[k]

            def TRB():
                k = 4 + tri[0] % 3
                tri[0] += 1
                return pbank[k], pbuf[k]

            def attn(h, qn, Bqn, qr, Bqr, mg, Bmg):
                p, Bp = PS()
                S.op("pe", MMS([(p[:, c * NS:(c + 1) * NS], [(wukT[:, h, c * 128:(c + 1) * 128], qn[:, :NS])]) for c in range(2)]),
                     reads=[Bc, Bqn], writes=[Bp])
                S.op("act", ACT(qlat[:, :, :, h * 4:(h + 1) * 4], p[:, :2 * NS].rearrange("p (c b t) -> p c b t", c=2, t=4)),
                     reads=[Bp, Bqlat], writes=[Bqlat])
                S.op("dve", CP(qrs[:, :, h * 4:(h + 1) * 4], qr[:, :NS].rearrange("p (b t) -> p b t", t=4)), reads=[Bqr, Bqrs], writes=[Bqrs])
                S.op("dve", CP(mgall[:, h, :], mg[:, :NS]), reads=[Bmg, Bmgall], writes=[Bmgall])
                if h < 7:
                    return
                items = []
                for b in range(NSQ):
                    items.append({"b": b, "ck": "new", "first": True, "last": False})
                    for ci in range(NCHK):
                        items.append({"b": b, "ck": "old", "ci": ci, "first": False, "last": ci == NCHK - 1})
                for k_, it_ in enumerate(items):
                    it_["conv"] = (k_ % 3 == 0)
                seqst = {}

                def stage_G(it):
                    b = it["b"]
                    if it.get("conv"):
                        conv_emit_some(1)
                    if it["first"]:
                        pix, Bpix = PIX.get()
                        S.dma("sp", DMA(pix[:PK, 0:1], ptab[0:1, b * NPG:(b + 1) * NPG].rearrange("o (p q) -> (o p) q", q=1)), writes=[Bpix])
                        S.op("dve", TS(pix[:PK, 0:1], pix[:PK, 0:1], NCHK, None, ALU.mult), reads=[Bpix], writes=[Bpix])
                        fm, Bfm = FST.get()
                        oa, Boa = FO.get()
                        seqst[b] = dict(pix=pix, Bpix=Bpix, fm=fm, Bfm=Bfm, oa=oa, Boa=Boa)
                        return
                    stq = seqst[b]
                    pix, Bpix = stq["pix"], stq["Bpix"]
                    cg, Bcg = CG.get()
                    rg_, Brg = RG.get()
                    r0 = it["ci"] * RC
                    S.dma("pool", lambda e, cg=cg, pix=pix, r0=r0: e.indirect_dma_start(
                        out=cg[:PK, :, :].rearrange("p r c -> p (r c)"), out_offset=None, in_=cck,
                        in_offset=bass.IndirectOffsetOnAxis(ap=pix[:PK, 0:1], axis=0), element_offset=r0 * 256), reads=[Bpix, Bcg], writes=[Bcg])
                    S.dma("pool", lambda e, rg_=rg_, pix=pix, r0=r0: e.indirect_dma_start(
                        out=rg_[:PK, :, :].rearrange("p r c -> p (r c)"), out_offset=None, in_=ckr,
                        in_offset=bass.IndirectOffsetOnAxis(ap=pix[:PK, 0:1], axis=0), element_offset=r0 * 64), reads=[Bpix, Brg], writes=[Brg])
                    it.update(cg=cg, Bcg=Bcg, rg=rg_, Brg=Brg)

                def stage_T(it):
                    b = it["b"]
                    if it["ck"] == "new":
                        nk = NS
                        CT_c = [ckTs[:, 0, :], ckTs[:, 1, :]]
                        RT_ = krTs[:, :]
                        rdk = [Bcks]
                        it.update(vals=[(ckS[:NS, :], NS)], rdv=[Bcks], kts=[(0, NS)])
                    else:
                        nk = NKC
                        cg, Bcg, rg_, Brg = it["cg"], it["Bcg"], it["rg"], it["Brg"]
                        ct, Bct = CTP.get()
                        ct1, Bct1 = CTP1.get()
                        rt, Brt = RTP.get()
                        for c in range(2):
                            p, Bp = TRB()
                            pv = bfv(p)
                            S.op("pe", TR([(pv[:, r * PK:(r + 1) * PK], cg[:PK, r, c * 128:(c + 1) * 128]) for r in range(RC)], idb),
                                 reads=[Bcg, Bc], writes=[Bp])
                            if c == 0:
                                S.op("act", ACT(ct[:, :], pv[:, :NKC]), reads=[Bp], writes=[Bct])
                            else:
                                S.op("dve", CP(ct1[:, :], pv[:, :NKC]), reads=[Bp], writes=[Bct1])
                        p, Bp = TRB()
                        pv = bfv(p)
                        S.op("pe", TR([(pv[:64, r * PK:(r + 1) * PK], rg_[:PK, r, :]) for r in range(RC)], idb), reads=[Brg, Bc], writes=[Bp])
                        S.op("act", ACT(rt[:, :], pv[:64, :NKC]), reads=[Bp], writes=[Brt])
                        CT_c = [ct[:, :], ct1[:, :]]
                        RT_ = rt[:, :]
                        rdk = [Bct, Bct1, Brt]
                        it.update(vals=[(cg[:PK, r, :], PK) for r in range(RC)], rdv=[Bcg], kts=[(r * PK, PK) for r in range(RC)])
                    kgs = splits(nk, 512)
                    s1, Bs1 = SM1.get()
                    pss = []
                    for gi, (ko, kn_) in enumerate(kgs):
                        p, Bp = SCB()
                        pss.append((p, Bp, ko, kn_))
                        S.op("pe", MM(p[:32, :kn_], [(qlat[:, 0, b, :], CT_c[0][:, ko:ko + kn_]), (qlat[:, 1, b, :], CT_c[1][:, ko:ko + kn_]),
                                                      (qrs[:, b, :], RT_[:, ko:ko + kn_])]), reads=[Bqlat, Bqrs] + rdk, writes=[Bp])
                        if it["ck"] == "new":
                            S.op("dve", TT_(p[:32, :kn_], p[:32, :kn_], smk[:, b, :], ALU.add), reads=[Bp, Bc], writes=[Bp])
                        S.op("dve", RED(s1[:32, gi:gi + 1], p[:32, :kn_], ALU.max), reads=[Bp, Bs1], writes=[Bs1])
                    it.update(pss=pss, s1=s1, Bs1=Bs1, nk=nk)

                def stage_X(it):
                    b = it["b"]
                    stq = seqst[b]
                    fm, Bfm, oacc_, Boa = stq["fm"], stq["Bfm"], stq["oa"], stq["Boa"]
                    if it["first"]:
                        S.op("dve", MSET(fm[:, 0:1], -1.0e4), reads=[Bfm], writes=[Bfm])
                        S.op("dve", MSET(fm[:, 1:2], 0.0), reads=[Bfm], writes=[Bfm])
                        S.op("dve", MSET(oacc_[:], 0.0), reads=[Boa], writes=[Boa])
                    pss, s1, Bs1, nk = it["pss"], it["s1"], it["Bs1"], it["nk"]
                    ng = len(pss)
                    s2_, Bs2 = SM1.get()
                    S.op("dve", RED(s2_[:32, 0:1], s1[:32, 0:ng], ALU.max), reads=[Bs1], writes=[Bs2])
                    S.op("dve", TT_(s2_[:32, 1:2], s2_[:32, 0:1], fm[:, 0:1], ALU.max), reads=[Bs2, Bfm], writes=[Bs2])
                    S.op("dve", TS(s2_[:32, 2:3], s2_[:32, 1:2], -SCALE, None, ALU.mult), reads=[Bs2], writes=[Bs2])
                    S.op("act", ACT(s2_[:32, 3:4], fm[:, 0:1], AF.Exp, scale=SCALE, bias=s2_[:32, 2:3]), reads=[Bs2, Bfm], writes=[Bs2])
                    S.op("dve", CP(fm[:, 0:1], s2_[:32, 1:2]), reads=[Bs2, Bfm], writes=[Bfm])
                    pm, Bpm = PMS.get()
                    s3, Bs3 = SM1.get()
                    for gi, (p, Bp, ko, kn_) in enumerate(pss):
                        S.op("act", ACT(pm[:, ko:ko + kn_], p[:32, :kn_], AF.Exp, scale=SCALE, bias=s2_[:32, 2:3], accum_out=s3[:32, gi:gi + 1]),
                             reads=[Bp, Bs2, Bs3, Bpm], writes=[Bpm, Bs3])
                    S.op("dve", RED(s3[:32, 4:5], s3[:32, 0:ng], ALU.add), reads=[Bs3], writes=[Bs3])
                    S.op("dve", STT(fm[:, 1:2], fm[:, 1:2], s2_[:32, 3:4], s3[:32, 4:5], ALU.mult, ALU.add), reads=[Bfm, Bs2, Bs3], writes=[Bfm])
                    it.update(pm=pm, Bpm=Bpm, s2_=s2_, Bs2=Bs2)

                def stage_XB(it):
                    pm, Bpm = it["pm"], it["Bpm"]
                    p7 = pbank[7]
                    pv = bfv(p7)
                    pts, Bpts = PTS.get()
                    kts = it["kts"]
                    S.op("pe", TR([(pv[:kn_, i * 32:(i + 1) * 32], pm[:, ko:ko + kn_]) for i, (ko, kn_) in enumerate(kts)], idb), reads=[Bpm, Bc], writes=[pbuf[7]])
                    kmax = max(kn_ for (_, kn_) in kts)
                    S.op("dve", CP(pts[:kmax, :len(kts) * 32], pv[:kmax, :len(kts) * 32]), reads=[pbuf[7]], writes=[Bpts])
                    it.update(pts=pts, Bpts=Bpts)

                def stage_X2(it):
                    b = it["b"]
                    stq = seqst[b]
                    fm, Bfm, oacc_, Boa = stq["fm"], stq["Bfm"], stq["oa"], stq["Boa"]
                    pts, Bpts, s2_, Bs2 = it["pts"], it["Bpts"], it["s2_"], it["Bs2"]
                    p7 = pbank[7]
                    S.op("pe", MM(p7[:32, 256:512], [(pts[:vn, i * 32:(i + 1) * 32], vap) for i, (vap, vn) in enumerate(it["vals"])]),
                         reads=[Bpts] + it["rdv"], writes=[BPV])
                    S.op("dve", STT(oacc_[:, :], oacc_[:, :], s2_[:32, 3:4], p7[:32, 256:512], ALU.mult, ALU.add), reads=[Boa, Bs2, BPV], writes=[Boa])
                    if it["last"]:
                        S.op("dve", RECIP(fm[:, 2:3], fm[:, 1:2]), reads=[Bfm], writes=[Bfm])
                        ob, Bob = OLB.get()
                        S.op("dve", TS(ob[:, :], oacc_[:, :], fm[:, 2:3], None, ALU.mult), reads=[Boa, Bfm], writes=[Bob])
                        p, Bp = TRB()
                        pv2 = bfv(p)
                        S.op("pe", TR([(pv2[:, c * 32:(c + 1) * 32], ob[:, c * 128:(c + 1) * 128]) for c in range(2)], idb), reads=[Bob, Bc], writes=[Bp])
                        S.op("act", ACT(olT[:, :, :, b * 4:(b + 1) * 4], pv2[:, 0:64].rearrange("p (c h t) -> p c h t", c=2, t=4)), reads=[Bp, BolT], writes=[BolT])

                n_it = len(items)
                for step in range(n_it + 3):
                    if step < n_it:
                        stage_G(items[step])
                    if 0 <= step - 3 < n_it:
                        stage_XB(items[step - 3])
                    if 0 <= step - 1 < n_it:
                        stage_T(items[step - 1])
                    if 0 <= step - 2 < n_it:
                        stage_X(items[step - 2])
                    if 0 <= step - 3 < n_it:
                        stage_X2(items[step - 3])
                if BPV.lw is not None:
                    pbuf[7].rd.extend([BPV.lw] + BPV.rd)
                for hh in range(8):
                    p, Bp = PS()
                    S.op("pe", MM(p[:, :NS], [(wukv[:, c, hh * 256 + 128:hh * 256 + 256], olT[:, c, hh, :]) for c in range(2)]), reads=[Bc, BolT], writes=[Bp])
                    S.op("dve", TT_(moT[:, hh, :NS], p[:, :NS], mgall[:, hh, :], ALU.mult), reads=[Bp, Bmgall], writes=[BmoT[hh]])
            return attn

        with phase():
            X = {}
            X["dms"] = cload([NS, 4, NS], F32, k_dms); X["qds"] = cload([128, 4, NS], F32, k_qds)
            X["kds"] = cload([NS, 4], F32, k_kds)
            X["bm"] = cload([128, NSQ, NS], F32, k_bm); X["rm"] = cload([NS, NSQ], F32, k_rm)
            X["sm"] = cload([32, NSQ, NS], F32, k_sm)
            X["io"] = cload([128, 1], I32, k_io)
            X["ckTs"] = sb([128, 2, NS], BF16); X["krTs"] = sb([64, NS], BF16); X["ckS"] = sb([NS, 256], BF16)
            X["Bcks"] = Buf("cks")
            wukT = sb([128, 8, 256], BF16)
            for h in range(8):
                p, Bp = PS()
                pv = bfv(p)
                S.op("pe", TR([(pv[:, c * 128:(c + 1) * 128], wukv[:, c, h * 256:h * 256 + 128]) for c in range(2)], idb),
                     reads=[Bc], writes=[Bp])
                S.op("act", ACT(wukT[:, h, :], pv[:, 0:256]), reads=[Bp, Bc], writes=[Bc])
            X["wukT"] = wukT
            block("s", 0, NS, X)
        with phase():
            X = {}
            X["ckT"] = sb([128, 2, SEQ], BF16); X["BckT"] = [Buf() for _ in range(SEQ // 128)]
            X["krT"] = sb([64, SEQ], BF16); X["BkrT"] = [Buf() for _ in range(SEQ // 128)]
            X["Sst"] = sb([128, 4, 2, 512], F32); X["BSst"] = [Buf(f"S{h}") for h in range(4)]
            X["Sbf"] = sb([128, 4, 2, 512], BF16); X["BSbf"] = [Buf(f"Sb{h}") for h in range(4)]
            for t0 in range(0, SEQ, TB):
                block("p", t0, TB, X)
        S.emit()
    return nc


def host_consts(SEQ, NSQ, NPG):
    NS = NSQ * 4
    TT = SEQ + NS
    past = NPG * 128
    pos = np.concatenate([np.arange(SEQ, dtype=np.float32),
                          np.tile(past + np.arange(4, dtype=np.float32), NSQ)]).astype(np.float32)
    c = {}
    invR = (10000.0 ** (-np.arange(128, dtype=np.float32) / 128)).astype(np.float32)
    angR = (pos[None, :] * invR[:, None]).astype(np.float32)
    c["k_cR"] = np.cos(angR).astype(np.float32); c["k_sR"] = np.sin(angR).astype(np.float32)
    invM = (10000.0 ** (-np.arange(32, dtype=np.float32) / 32)).astype(np.float32)
    angM = (pos[None, :] * invM[:, None]).astype(np.float32)
    cm_, sm_ = np.cos(angM).astype(np.float32), np.sin(angM).astype(np.float32)
    c["k_c2"] = np.concatenate([cm_, cm_], 0); c["k_s2"] = np.concatenate([-sm_, sm_], 0)
    c["k_cM"] = np.ascontiguousarray(cm_.T); c["k_sM"] = np.ascontiguousarray(sm_.T)
    c["k_id"] = np.eye(128, dtype=np.float32)
    lg = np.log1p(-np.exp2(-5.0 - np.arange(4, dtype=np.float64)))
    i = np.arange(128)
    diff = i[None, :] - i[:, None]
    dm = np.zeros((128, 4, 128), np.float32)
    for h in range(4):
        dm[:, h, :] = np.where(diff >= 0, np.exp(np.maximum(diff, 0) * lg[h]), 0.0)
    c["k_dm"] = dm
    tt = np.arange(NS) % 4
    bb = np.arange(NS) // 4
    dms = np.zeros((NS, 4, NS), np.float32)
    for h in range(4):
        d = tt[None, :] - tt[:, None]
        dms[:, h, :] = np.where((bb[None, :] == bb[:, None]) & (d >= 0), np.exp(np.maximum(d, 0) * lg[h]), 0.0)
    c["k_dms"] = dms
    c["k_qd"] = np.broadcast_to(np.exp((i[None, :] + 1.0) * lg[:, None])[None], (128, 4, 128)).astype(np.float32).copy()
    c["k_qds"] = np.broadcast_to(np.exp((tt[None, :] + 1.0) * lg[:, None])[None], (128, 4, NS)).astype(np.float32).copy()
    c["k_kd"] = np.exp((127.0 - i)[:, None] * lg[None, :]).astype(np.float32)
    c["k_kds"] = np.exp((3.0 - tt)[:, None] * lg[None, :]).astype(np.float32)
    bmk = (bb[None, :] == np.arange(NSQ)[:, None]).astype(np.float32)
    c["k_bm"] = np.broadcast_to(bmk[None], (128, NSQ, NS)).copy()
    c["k_rm"] = np.ascontiguousarray(bmk.T)
    c["k_cm"] = np.where(i[None, :] <= i[:, None], 0.0, NEG).astype(np.float32)
    tq = np.arange(32) % 4
    smk = np.full((32, NSQ, NS), NEG, np.float32)
    for b in range(NSQ):
        ok = (bb[None, :] == b) & (tt[None, :] <= tq[:, None])
        smk[:, b, :] = np.where(ok, 0.0, NEG)
    c["k_sm"] = smk
    c["k_io"] = np.arange(128, dtype=np.int32)[:, None]
    return c


_CACHE = {}


def run(inputs, SEQ, NSQ, NPG, NPOOL, TB, ncores):
    key = (SEQ, NSQ, NPG, NPOOL, TB)
    if key not in _CACHE:
        _CACHE[key] = build(*key)
    nc = _CACHE[key]
    f = lambda a: np.ascontiguousarray(np.asarray(a, dtype=np.float32))
    consts = host_consts(SEQ, NSQ, NPG)
    cck = f(inputs["cache_ckv"][0]).reshape(NPOOL * 16, 8 * 256)
    ckr = f(inputs["cache_krope"][0]).reshape(NPOOL * 16, 8 * 64)
    shared = dict(
        cck=cck, ckr=ckr,
        ln_w=f(inputs["ln_w"][0]), w_in=f(inputs["w_in"][0]), q_norm_w=f(inputs["q_norm_w"][0]), w_uq=f(inputs["w_uq"][0]),
        kv_norm_w=f(inputs["kv_norm_w"][0]).reshape(1, 256), w_ukv=f(inputs["w_ukv"][0]), ret_gn_w=f(inputs["ret_gn_w"][0]),
        w_down_ret=f(inputs["w_down_ret"][0]), w_down_mla=f(inputs["w_down_mla"][0]), w_out=f(inputs["w_out"][0]),
        w_ple_gate=f(inputs["w_ple_gate"][0]), w_ple_proj=f(inputs["w_ple_proj"][0]), fin_w=f(inputs["final_norm_w"]).reshape(1, D),
        **consts)
    in_maps = []
    NS = NSQ * 4
    for c in range(ncores):
        m = dict(shared)
        m["xp"] = f(inputs["x_prompt"][c])
        m["xs"] = f(inputs["x_sample"][c * NSQ:(c + 1) * NSQ]).reshape(NS, D)
        m["stt"] = f(inputs["state_ret"][0, c * NSQ:(c + 1) * NSQ]).reshape(NSQ * 4 * 256, 512)
        m["ptab"] = np.ascontiguousarray(np.asarray(inputs["page_table"][c * NSQ:(c + 1) * NSQ], dtype=np.int32)).reshape(1, NSQ * NPG)
        m["ppl"] = f(inputs["p_prompt"][0, c])
        m["psl"] = f(inputs["p_sample"][0, c * NSQ:(c + 1) * NSQ]).reshape(NS, 256)
        in_maps.append(m)
    res = run_bass_kernel_spmd(nc, in_maps, core_ids=list(range(ncores)))
    R = res.results
    cat = lambda k: np.stack([R[c][k] for c in range(ncores)], 0)
    y_p = cat("yp")
    y_s = cat("ys").reshape(ncores * NSQ, 4, D)
    ckv_p = cat("ockp")[None]
    kr_p = cat("okrp")[None]
    ret_p = cat("oretp").reshape(1, ncores, 4, 256, 512)
    ckv_s = cat("ocks").reshape(1, ncores * NSQ, 4, 256)
    kr_s = cat("okrs").reshape(1, ncores * NSQ, 4, 64)
    ret_s = cat("orets").reshape(1, ncores * NSQ, 4, 256, 512)
    return (y_p, y_s, ckv_p, kr_p, ret_p, ckv_s, kr_s, ret_s)


def kernel(**inputs):
    B, SEQ, _ = inputs["x_prompt"].shape
    DB = inputs["x_sample"].shape[0]
    NPG = inputs["page_table"].shape[1]
    NPOOL = inputs["cache_ckv"].shape[1]
    ncores = 8
    assert B == ncores
    return run(inputs, SEQ, DB // ncores, NPG, NPOOL, min(512, SEQ), ncores)
```
